# Optimizing a Trainium2 kernel written in Bass

```python
import jax, jax.numpy as jnp
from jax import lax
import numpy as np

D_MODEL = 2048
BATCH = 4
SEQ = 4096
DEPTH = 4

N_A = DEPTH // 2
N_B = DEPTH - N_A
POOL_WINDOWS = (2, 4, 8, 16)
N_POOL_GROUPS = len(POOL_WINDOWS)
POOL_GROUP_DIM = D_MODEL // N_POOL_GROUPS
QK_NOPE_DIM = 128
QK_ROPE_DIM = 64
V_HEAD_DIM = 128
N_HEADS = D_MODEL // V_HEAD_DIM
KV_LORA_RANK = D_MODEL // 4
Q_LORA_RANK = ((1536 * D_MODEL // 7168 + 127) // 128) * 128
Q_HEAD_DIM = QK_NOPE_DIM + QK_ROPE_DIM
SM_SCALE = Q_HEAD_DIM ** -0.5
ROPE_THETA = 10000.0
Q_BLOCK = 128
FFN_DIM = ((8 * D_MODEL // 3 + 255) // 256) * 256
N_MOD = 6
EPS = 1e-6

kernel_name = "yoco_pool_mla_adaln_trunk"


def rmsnorm(x, g):
    x32 = x.astype(jnp.float32)
    y = x32 * lax.rsqrt(jnp.mean(x32 * x32, axis=-1, keepdims=True) + EPS)
    return (y * g.astype(jnp.float32)).astype(x.dtype)


def modulate(h, shift, scale):
    return h * (1.0 + scale[:, None, :]) + shift[:, None, :]


def rope_tables(positions):
    inv_freq = 1.0 / (ROPE_THETA ** (jnp.arange(0, QK_ROPE_DIM, 2, dtype=jnp.float32) / QK_ROPE_DIM))
    ang = positions.astype(jnp.float32)[..., None] * inv_freq
    return jnp.cos(ang), jnp.sin(ang)


def apply_rope(t, cos, sin):
    half = t.shape[-1] // 2
    t1, t2 = t[..., :half], t[..., half:]
    return jnp.concatenate([t1 * cos - t2 * sin, t2 * cos + t1 * sin], axis=-1).astype(t.dtype)


def pool_mixer(h, w_grp, scale):
    B, S, D = h.shape
    hf = h.astype(jnp.float32)
    csum = jnp.cumsum(hf, axis=1)
    t = jnp.arange(S)
    outs = []
    for g, w in enumerate(POOL_WINDOWS):
        lo, hi = g * POOL_GROUP_DIM, (g + 1) * POOL_GROUP_DIM
        cs = csum[..., lo:hi]
        lag = jnp.pad(cs, ((0, 0), (w, 0), (0, 0)))[:, :S]
        cnt = jnp.minimum(t + 1, w).astype(jnp.float32)[None, :, None]
        outs.append((cs - lag) / cnt - hf[..., lo:hi])
    d = jnp.stack(outs, axis=2).astype(h.dtype)
    y = jnp.einsum('bsgc,gcd->bsgd', d, w_grp).reshape(B, S, D)
    return y * scale


def swiglu(h, w_gate, w_up, w_down):
    return (jax.nn.silu(h @ w_gate) * (h @ w_up)) @ w_down


def shared_kv(h, w_dkv, kv_norm, w_uk, w_uv, w_kr, cos, sin):
    B, S, _ = h.shape
    ckv = rmsnorm(h @ w_dkv, kv_norm)
    k_nope = (ckv @ w_uk).reshape(B, S, N_HEADS, QK_NOPE_DIM)
    v = (ckv @ w_uv).reshape(B, S, N_HEADS, V_HEAD_DIM)
    k_rope = apply_rope(h @ w_kr, cos, sin)
    return k_nope, k_rope, v


def causal_mla_attention(q_nope, q_rope, k_nope, k_rope, v):
    B, S, H, _ = q_nope.shape
    nblk = S // Q_BLOCK
    key_idx = jnp.arange(S)

    def to_blocks(t):
        return jnp.moveaxis(t.reshape((B, nblk, Q_BLOCK) + t.shape[2:]), 1, 0)

    def one_block(args):
        qn, qr, blk = args
        s = (jnp.einsum('bqhd,bkhd->bhqk', qn, k_nope, preferred_element_type=jnp.float32)
             + jnp.einsum('bqhr,bkr->bhqk', qr, k_rope, preferred_element_type=jnp.float32))
        q_idx = blk * Q_BLOCK + jnp.arange(Q_BLOCK)
        mask = key_idx[None, :] <= q_idx[:, None]
        p = jax.nn.softmax(jnp.where(mask, s * SM_SCALE, -jnp.inf), axis=-1)
        return jnp.einsum('bhqk,bkhd->bqhd', p.astype(v.dtype), v)

    o = lax.map(one_block, (to_blocks(q_nope), to_blocks(q_rope), jnp.arange(nblk)))
    return jnp.moveaxis(o, 0, 1).reshape(B, S, H * V_HEAD_DIM)


def mla_mixer(h, kv, w_dq, q_norm, w_uq, w_o, cos, sin):
    B, S, _ = h.shape
    cq = rmsnorm(h @ w_dq, q_norm)
    q = (cq @ w_uq).reshape(B, S, N_HEADS, Q_HEAD_DIM)
    q_nope = q[..., :QK_NOPE_DIM]
    q_rope = apply_rope(q[..., QK_NOPE_DIM:], cos[:, :, None, :], sin[:, :, None, :])
    k_nope, k_rope, v = kv
    o = causal_mla_attention(q_nope, q_rope, k_nope, k_rope, v)
    return o @ w_o


def setup_inputs(seed: int = 0) -> dict:
    key = jax.random.key(seed)
    ks = jax.random.split(key, 32)
    f32 = jnp.float32

    def nrm(k, shape, std):
        return jax.random.normal(k, shape, f32) * std

    def gain(k, shape):
        return 1.0 + 0.02 * jax.random.normal(k, shape, f32)

    D, F, H = D_MODEL, FFN_DIM, N_HEADS
    positions = (jax.random.randint(ks[2], (BATCH, 1), 0, 1024, dtype=jnp.int32)
                 + jnp.arange(SEQ, dtype=jnp.int32)[None, :])
    return {
        "x": nrm(ks[0], (BATCH, SEQ, D), 1.0),
        "c": nrm(ks[1], (BATCH, D), 1.0),
        "positions": positions,
        "mod_w": nrm(ks[3], (DEPTH, D, N_MOD * D), 0.5 * D ** -0.5),
        "mod_b": nrm(ks[4], (DEPTH, N_MOD * D), 0.02),
        "norm_mix": gain(ks[5], (DEPTH, D)),
        "norm_ffn": gain(ks[6], (DEPTH, D)),
        "pool_w": nrm(ks[7], (N_A, N_POOL_GROUPS, POOL_GROUP_DIM, POOL_GROUP_DIM), POOL_GROUP_DIM ** -0.5),
        "pool_scale": gain(ks[8], (N_A, D)),
        "kv_mod_w": nrm(ks[9], (D, 2 * D), 0.5 * D ** -0.5),
        "kv_mod_b": nrm(ks[10], (2 * D,), 0.02),
        "kv_in_norm": gain(ks[11], (D,)),
        "w_dkv": nrm(ks[12], (D, KV_LORA_RANK), D ** -0.5),
        "kv_norm": gain(ks[13], (KV_LORA_RANK,)),
        "w_uk": nrm(ks[14], (KV_LORA_RANK, H * QK_NOPE_DIM), KV_LORA_RANK ** -0.5),
        "w_uv": nrm(ks[15], (KV_LORA_RANK, H * V_HEAD_DIM), KV_LORA_RANK ** -0.5),
        "w_kr": nrm(ks[16], (D, QK_ROPE_DIM), D ** -0.5),
        "w_dq": nrm(ks[17], (N_B, D, Q_LORA_RANK), D ** -0.5),
        "q_norm": gain(ks[18], (N_B, Q_LORA_RANK)),
        "w_uq": nrm(ks[19], (N_B, Q_LORA_RANK, H * Q_HEAD_DIM), Q_LORA_RANK ** -0.5),
        "w_o": nrm(ks[20], (N_B, H * V_HEAD_DIM, D), (H * V_HEAD_DIM) ** -0.5),
        "ffn_gate": nrm(ks[21], (DEPTH, D, F), D ** -0.5),
        "ffn_up": nrm(ks[22], (DEPTH, D, F), D ** -0.5),
        "ffn_down": nrm(ks[23], (DEPTH, F, D), F ** -0.5),
        "final_norm": gain(ks[24], (D,)),
    }


def reference(x, c, positions, mod_w, mod_b, norm_mix, norm_ffn, pool_w, pool_scale,
              kv_mod_w, kv_mod_b, kv_in_norm, w_dkv, kv_norm, w_uk, w_uv, w_kr,
              w_dq, q_norm, w_uq, w_o, ffn_gate, ffn_up, ffn_down, final_norm):
    cos, sin = rope_tables(positions)
    sc = jax.nn.silu(c)
    kv = None
    for i in range(DEPTH):
        shift_m, scale_m, gate_m, shift_f, scale_f, gate_f = jnp.split(sc @ mod_w[i] + mod_b[i], N_MOD, axis=-1)
        if i == N_A:
            kv_shift, kv_scale = jnp.split(sc @ kv_mod_w + kv_mod_b, 2, axis=-1)
            h_kv = modulate(rmsnorm(x, kv_in_norm), kv_shift, kv_scale)
            kv = shared_kv(h_kv, w_dkv, kv_norm, w_uk, w_uv, w_kr, cos, sin)
        h = modulate(rmsnorm(x, norm_mix[i]), shift_m, scale_m)
        if i < N_A:
            y = pool_mixer(h, pool_w[i], pool_scale[i])
        else:
            j = i - N_A
            y = mla_mixer(h, kv, w_dq[j], q_norm[j], w_uq[j], w_o[j], cos, sin)
        x = x + gate_m[:, None, :] * y
        h = modulate(rmsnorm(x, norm_ffn[i]), shift_f, scale_f)
        x = x + gate_f[:, None, :] * swiglu(h, ffn_gate[i], ffn_up[i], ffn_down[i])
    return rmsnorm(x, final_norm)
```

```python
import contextlib
import numpy as np
import ml_dtypes
import concourse.bass as bass
import concourse.mybir as mybir
from concourse.bass_utils import run_bass_kernel_spmd

F32 = mybir.dt.float32
BF16 = mybir.dt.bfloat16
I32 = mybir.dt.int32
AF = mybir.ActivationFunctionType
ALU = mybir.AluOpType

NCORES = 8
B = 4
S = 4096
D = 2048
KC = 16
F = 5632
FCN = 44
T = 512
HALO = 32
TW = T + HALO
NT = 4
H = 16
EPS = 1e-6
SM_SCALE = 192.0 ** -0.5
BLOCKS = {0: [0, 3, 4, 7], 1: [1, 2, 5, 6]}
NRING = 7
RING_ELEMS = 4096
TWO_PI = float(2.0 * np.pi)


def vec_layout():
    off = {}
    n = 0

    def add(name, w):
        nonlocal n
        off[name] = (n, w)
        n += w

    for l in range(4):
        add(f"modb{l}", 96)
    add("kvmodb", 32)
    for l in range(4):
        add(f"nmix{l}", 16)
        add(f"nffn{l}", 16)
    for l in range(2):
        add(f"pscale{l}", 16)
    add("kvin", 16)
    add("kvn", 4)
    add("qn0", 4)
    add("qn1", 4)
    add("fin", 16)
    add("invf", 1)
    add("sgn", 1)
    return off, n


VOFF, NV = vec_layout()
DBG = {}


class Sem:
    def __init__(self, h):
        self.h = h


class Dep:
    __slots__ = ("w", "r")

    def __init__(self):
        self.w = None
        self.r = []


class Ctx:
    ENG = ("pe", "act", "dve", "pool", "sp")

    def __init__(self, nc, stack):
        self.nc = nc
        self.stack = stack
        self.q = {e: [] for e in self.ENG}
        self.sem = {}
        self.cnt = {}
        self.waited = {e: {} for e in self.ENG}
        self.nsem = 0
        self.deps = {}
        self.const = set()
        self.dsems = {}
        self.pending = {e: [] for e in self.ENG}
        for e in self.ENG:
            self._rot(e)

    def newsem(self, name):
        self.nsem += 1
        return Sem(self.stack.enter_context(self.nc.semaphore(f"{name}_{self.nsem}")))

    def _rot(self, e):
        self.sem[e] = self.newsem("s" + e)
        self.cnt[e] = 0

    def dsem(self, name):
        if name not in self.dsems:
            self.dsems[name] = [self.newsem("d"), 0]
        return self.dsems[name]

    def barrier(self, tok):
        for e in self.ENG:
            self.pending[e].append(tok)

    def emit(self, e, fn, waits=(), sig=True, dma=None):
        ws = []
        if self.pending[e]:
            waits = list(waits) + self.pending[e]
            self.pending[e] = []
        mx = {}
        for t in waits:
            if t is None:
                continue
            s, v = t
            k = id(s)
            if k not in mx or mx[k][1] < v:
                mx[k] = (s, v)
        for k, (s, v) in mx.items():
            if self.waited[e].get(k, 0) >= v:
                continue
            self.waited[e][k] = v
            ws.append((s, v))
        tok = None
        inc = 1
        if dma is not None:
            dma[1] += 16
            tok = (dma[0], dma[1])
            inc = 16
        elif sig:
            if self.cnt[e] >= 30000:
                self._rot(e)
            self.cnt[e] += 1
            tok = (self.sem[e], self.cnt[e])
        self.q[e].append((fn, ws, tok, inc))
        return tok

    def op(self, e, fn, reads=(), writes=(), extra=(), sig=True, dma=None):
        waits = list(extra)
        for n in reads:
            d = self.deps.setdefault(n, Dep())
            if d.w is not None:
                waits.append(d.w)
        for n in writes:
            d = self.deps.setdefault(n, Dep())
            if d.w is not None:
                waits.append(d.w)
            waits.extend(d.r)
        tok = self.emit(e, fn, waits, sig, dma)
        if tok is not None:
            for n in reads:
                if n not in self.const:
                    self.deps[n].r.append(tok)
            for n in writes:
                d = self.deps[n]
                d.w = tok
                d.r = []
        return tok

    def run(self, block):
        def mk(e):
            def body(eng):
                for fn, ws, tok, inc in self.q[e]:
                    for s, v in ws:
                        eng.wait_ge(s.h, v)
                    ins = fn(eng)
                    if tok is not None:
                        ins.then_inc(tok[0].h, inc)
                if e == "sp":
                    for name, (s, c) in self.dsems.items():
                        if c > 0:
                            eng.wait_ge(s.h, c)
            return body

        block.tensor(mk("pe"))
        block.scalar(mk("act"))
        block.vector(mk("dve"))
        block.gpsimd(mk("pool"))
        block.sync(mk("sp"))


class Psum:
    def __init__(self, banks):
        self.banks = banks
        self.held = [False] * len(banks)
        self.i = 0

    def alloc(self):
        n = len(self.banks)
        for _ in range(n):
            k = self.i % n
            self.i += 1
            if not self.held[k]:
                self.held[k] = True
                return k
        raise RuntimeError("psum exhausted")

    def free(self, k):
        self.held[k] = False


class Ring:
    def __init__(self, ctx, slots):
        self.ctx = ctx
        self.slots = slots
        self.i = 0
        self.gen = [0] * len(slots)

    def load(self, src, a, b, npart=128, reads=()):
        k = self.i % len(self.slots)
        self.i += 1
        self.gen[k] += 1
        assert a * b <= RING_ELEMS
        flat = self.slots[k][0:npart, 0:a * b]
        view = flat.rearrange("p (a b) -> p a b", a=a) if a > 1 else flat
        dst = view
        self.ctx.op("pool", lambda g, d=dst, s=src: g.dma_start(out=d, in_=s),
                    reads=list(reads), writes=[("ring", k)], dma=self.ctx.dsem(("ring", k)))
        return ("ring", k), view

    def load_multi(self, a, b, mk_pieces, reads=(), npart=128):
        ctx = self.ctx
        k = self.i % len(self.slots)
        self.i += 1
        flat = self.slots[k][0:npart, 0:a * b]
        view = flat.rearrange("p (a b) -> p a b", a=a)
        name = ("ring", k)
        d = ctx.deps.setdefault(name, Dep())
        waits = [d.w] + list(d.r)
        for n in reads:
            dn = ctx.deps.setdefault(n, Dep())
            waits.append(dn.w)
        tok = None
        for i, (dst, src) in enumerate(mk_pieces(view)):
            tok = ctx.emit("pool", lambda g, d_=dst, s_=src: g.dma_start(out=d_, in_=s_), waits if i == 0 else (),
                           dma=ctx.dsem(name))
        d.w = tok
        d.r = []
        return name, view


def build(phase):
    doA = phase in ("A", "F")
    doB = phase in ("B", "F")
    nc = bass.Bass("TRN2", target_bir_lowering=False)

    def din(name, shape, dt=F32):
        return nc.dram_tensor(name, list(shape), dt, kind="ExternalInput").ap()

    def dout(name, shape, dt=F32):
        return nc.dram_tensor(name, list(shape), dt, kind="ExternalOutput").ap()

    cT_d = din("cT", [128, KC])
    modw_d = din("modw", [D, 208 * 128])
    modin_d = nc.dram_tensor("modin", [128, 208], F32, kind="Internal").ap()
    modout_d = nc.dram_tensor("modout", [2 * 128, 208], F32, kind="Internal").ap()
    vecs_d = din("vecs", [128, NV])
    posb_d = din("posb", [NT, 64, T], I32)
    ffn_gate = din("ffn_gate", [4, D, F])
    ffn_up = din("ffn_up", [4, D, F])
    ffn_down = din("ffn_down", [4, F, D])
    if doA:
        xin_d = din("xin", [NT, 128, KC, TW])
        aux_d = din("aux", [128, 96])
        pool_w = din("pool_w", [2, 4, 512, 512])
        w_dkv = din("w_dkv", [D, 512])
        w_uk = din("w_uk", [512, D])
        w_uv = din("w_uv", [512, D])
        w_kr = din("w_kr_ext", [D, 128])
    assert phase == "F"
    if doB:
        mk_d = din("masks", [2, 128, 8, T], BF16)
        w_dq = din("w_dq", [2, D, 512])
        w_uq = din("w_uq_ext", [2, 512, H, 256])
        w_o = din("w_o", [2, D, D])
        out_d = dout("out", [NT, 128, KC, T])
    if phase == "F":
        x1_d = nc.dram_tensor("x1s", [NT, 128, KC, T], F32, kind="Internal").ap()
        kin = [nc.dram_tensor(f"kin{t}", [H * 128, T], BF16, kind="Internal").ap() for t in range(NT)]
        vin = [nc.dram_tensor(f"vin{t}", [H * 128, T], BF16, kind="Internal").ap() for t in range(NT)]
        rin = [nc.dram_tensor(f"rin{t}", [64, T], BF16, kind="Internal").ap() for t in range(NT)]
        kout = [nc.dram_tensor(f"kout{t}", [2 * H * 128, T], BF16, kind="Internal").ap() for t in range(NT)]
        vout = [nc.dram_tensor(f"vout{t}", [2 * H * 128, T], BF16, kind="Internal").ap() for t in range(NT)]
        rout = [nc.dram_tensor(f"rout{t}", [128, T], BF16, kind="Internal").ap() for t in range(NT)]
        PAIRS = [[0, 1], [2, 3], [4, 5], [6, 7]]

    layers = ([0, 1] if doA else []) + ([2, 3] if doB else [])

    with contextlib.ExitStack() as st:
        ctx = Ctx(nc, st)

        def sb(name, shape, dt):
            return st.enter_context(nc.sbuf_tensor("sb_" + name, list(shape), dt))

        ones = sb("ones", [128, 128], BF16)
        ones32 = sb("ones32", [128, 128], F32)
        vecs = sb("vecs", [128, NV], F32)
        cT = sb("cTs", [128, KC], F32)
        scall = sb("scall", [128, KC], BF16)
        modv = sb("modv", [128, 4 * 96 + 32], F32)
        der = sb("der", [128, 4 * 48 + 16], F32)
        xT = sb("xT", [128, KC, TW], F32)
        hT = sb("hT", [128, KC, TW], BF16)
        aTr = sb("aT", [128, 48, TW], BF16)
        rstd = sb("rstd", [128, TW], F32)
        tmps = [sb(f"tmp{i}", [128, TW], F32) for i in range(3)]
        sgs = [sb(f"sg{i}", [128, TW], F32) for i in range(2)]
        ring_slots = [sb(f"ring{i}", [128, RING_ELEMS], BF16) for i in range(NRING)]
        trig = sb("trig", [64, 2, T], F32)
        posi = sb("posi", [64, T], I32)
        if doA:
            aux = sb("aux", [128, 96], F32)
            validb = sb("validb", [128, HALO], BF16)
        if doB:
            masks = sb("masks", [128, 2, 8, T], BF16)
            KRb = sb("KRb", [64, RING_ELEMS], BF16)
        banks = [st.enter_context(nc.psum_tensor(f"ps{i}", [128, 512], F32)) for i in range(8)]
        psum = Psum(banks)
        ring = Ring(ctx, ring_slots)
        block = st.enter_context(nc.Block())

        sq = aTr
        S0 = aTr[:, 32:40, :].rearrange("p a b -> p (a b)").bitcast(F32).rearrange("p (a b) -> p a b", a=4)
        S1 = aTr[:, 40:48, :].rearrange("p a b -> p (a b)").bitcast(F32).rearrange("p (a b) -> p a b", a=4)

        def an(lo, hi):
            return [("a", i) for i in range(lo, hi)]

        def arow(r0, nr, nparts=128):
            return aTr[0:nparts, r0:r0 + nr, :].rearrange("p a b -> p (a b)")

        raw4 = arow(16, 8)[:, 0:4096].bitcast(F32).rearrange("p (a b) -> p a b", a=4)
        RAW4 = an(16, 24)
        nrmT = arow(24, 4)[:, 0:2048].rearrange("p (a b) -> p a b", a=4)
        NRMT = an(24, 28)
        kbufs = [arow(28 + i, 1)[:, 0:T] for i in range(3)]
        vbufs = [arow(31 + 4 * i, 4)[:, 0:D] for i in range(2)]
        krbuf = arow(39, 1, 64)[:, 0:T]
        qns = [arow(28 + i, 1)[:, 0:T] for i in range(2)]
        qrs = [arow(30 + i, 1, 64)[:, 0:T] for i in range(2)]
        pTs = [arow(32 + i, 1)[:, 0:T] for i in range(4)]
        rden = arow(36, 2)[:, 0:2 * T].bitcast(F32)
        oTs = arow(0, 16)[:, 0:H * T].rearrange("p (h t) -> p h t", h=H)
        ON = an(0, 16)
        rt = tmps[2]
        accs = [arow(38 + 2 * i, 2)[:, 0:2 * T].bitcast(F32) for i in range(2)]
        gall = arow(40, 7)[:, 0:2 * 2 * 208].bitcast(F32).rearrange("p (r x) -> p r x", r=2)
        gsb = arow(47, 1)[:, 0:416].bitcast(F32)

        XN = [("x", c) for c in range(KC)]

        def V(name, c=None):
            o, w = VOFF[name]
            if c is None:
                return vecs[:, o:o + w]
            return vecs[:, o + c:o + c + 1]

        ctx.op("dve", lambda v: v.memset(ones[:], 1.0), writes=["ones"])
        ctx.op("dve", lambda v: v.memset(ones32[:], 1.0), writes=["ones32"])
        ctx.op("sp", lambda s: s.dma_start(out=vecs[:], in_=vecs_d), writes=["vecs"], dma=ctx.dsem("vecs"))
        ctx.op("sp", lambda s: s.dma_start(out=cT[:], in_=cT_d), writes=["cT"], dma=ctx.dsem("cT"))
        if doA:
            ctx.op("sp", lambda s: s.dma_start(out=aux[:], in_=aux_d), writes=["aux"], dma=ctx.dsem("aux"))
            ctx.op("dve", lambda v: v.tensor_copy(out=validb[:], in_=aux[:, 0:HALO]), reads=["aux"], writes=["validb"])
            ctx.op("dve", lambda v: v.memset(aTr[:, 16:32, :], 0.0), writes=an(16, 32))
        if doB:
            ctx.op("sp", lambda s: s.dma_start(out=masks[:].rearrange("p m c t -> p m (c t)"),
                                               in_=mk_d.rearrange("m p c t -> p m (c t)")),
                   writes=["masks"], dma=ctx.dsem("masks"))
        ctx.op("act", lambda a: a.activation(out=scall[:], in_=cT[:], func=AF.Silu), reads=["cT"], writes=["scall"])
        for n in ("ones", "ones32", "vecs", "scall", "aux", "validb", "masks", "modv", "der"):
            ctx.const.add(n)

        def mm_group(items, reads, k_list):
            n = len(items)
            for i, (o, l, r, s0, s1) in enumerate(items):
                fn = (lambda pe, o=o, l=l, r=r, s0=s0, s1=s1: pe.matmul(o, l, r, start=s0, stop=s1))
                if n == 1:
                    ctx.op("pe", fn, reads=reads, writes=[("ps", k) for k in k_list])
                elif i == 0:
                    waits = []
                    for nm in reads:
                        d = ctx.deps.setdefault(nm, Dep())
                        if d.w is not None:
                            waits.append(d.w)
                    for k in k_list:
                        d = ctx.deps.setdefault(("ps", k), Dep())
                        if d.w is not None:
                            waits.append(d.w)
                        waits.extend(d.r)
                    ctx.emit("pe", fn, waits, sig=False)
                elif i == n - 1:
                    ctx.op("pe", fn, reads=reads, writes=[("ps", k) for k in k_list])
                else:
                    ctx.emit("pe", fn, sig=False)

        def mod_all():
            k = psum.alloc()
            ps = banks[k]
            wv = modw_d.rearrange("(kc p) m -> p kc m", p=128)
            for mb in range(104):
                rn, w = ring.load(wv[:, :, mb * 256:(mb + 1) * 256], KC, 256)
                for sub in range(2):
                    lc = mb * 2 + sub
                    items = [(ps[:, lc:lc + 1], w[:, kc, sub * 128:(sub + 1) * 128], scall[:, kc:kc + 1], kc == 0, kc == KC - 1)
                             for kc in range(KC)]
                    mm_group(items, [rn, "scall"], [k])
            ctx.op("dve", lambda v: v.tensor_copy(out=gsb, in_=ps[:, 0:208]), reads=[("ps", k)], writes=an(47, 48))
            psum.free(k)
            ctx.op("sp", lambda s_: s_.dma_start(out=modin_d, in_=gsb), reads=an(47, 48), writes=["modin"], dma=ctx.dsem("gsb"))
            ctx.op("pool", lambda g: g.collective_compute("AllGather", ALU.bypass, replica_groups=[[0, 1], [2, 3], [4, 5], [6, 7]],
                                                          ins=[modin_d], outs=[modout_d]), reads=["modin"], writes=["modout"])
            ctx.op("sp", lambda s_: s_.dma_start(out=gall, in_=modout_d.rearrange("(r p) x -> p r x", p=128)),
                   reads=["modout"], writes=an(40, 47), dma=ctx.dsem("gall"))
            ctx.op("dve", lambda v: v.tensor_tensor(out=modv[:, 0:416], in0=gall.rearrange("p r x -> p (r x)"), in1=vecs[:, 0:416], op=ALU.add),
                   reads=an(40, 47) + ["vecs"], writes=[("modv", 0)])

        def MV(l, part, c=None):
            o = l * 96 + part * 16
            if c is None:
                return modv[:, o:o + 16]
            return modv[:, o + c:o + c + 1]

        def DER(l, part, c=None):
            o = l * 48 + part * 16
            if c is None:
                return der[:, o:o + 16]
            return der[:, o + c:o + c + 1]

        def derive(l):
            for part, vn, sp_ in ((0, f"nmix{l}", 1), (1, f"nffn{l}", 4)):
                ctx.op("dve", lambda v, p=part, vn=vn, sp_=sp_: v.scalar_tensor_tensor(
                    out=DER(l, p), in0=MV(l, sp_), scalar=1.0, in1=V(vn), op0=ALU.add, op1=ALU.mult),
                    reads=[("modv", 0), "vecs"], writes=[("der", l, part)])
            if l < 2:
                ctx.op("dve", lambda v: v.tensor_tensor(out=DER(l, 2), in0=MV(l, 2), in1=V(f"pscale{l}"), op=ALU.mult),
                       reads=[("modv", 0), "vecs"], writes=[("der", l, 2)])

        mod_all()
        for l in layers:
            derive(l)
        ctx.op("dve", lambda v: v.scalar_tensor_tensor(
            out=der[:, 192:208], in0=modv[:, 400:416], scalar=1.0, in1=V("kvin"), op0=ALU.add, op1=ALU.mult),
            reads=[("modv", 0), "vecs"], writes=[("der", "kv")])
        ctx.barrier(ctx.op("dve", lambda v: v.memset(rt[:], 1.0), writes=[("tmp", 2)]))

        tmp_i = [0]

        def rmsnorm_mod(ranges, Afn, Bfn, xreads=None):
            lo_all, hi_all = ranges[0][0], ranges[-1][1]
            ctx.op("act", lambda a: a.activation(out=sq[:, 0:KC, lo_all:hi_all], in_=xT[:, :, lo_all:hi_all], func=AF.Square),
                   reads=XN, writes=an(0, KC))
            for (lo, hi) in ranges:
                k = psum.alloc()
                ps = banks[k]
                items = [(ps[:, 0:hi - lo], ones[:], sq[:, kc, lo:hi], kc == 0, kc == KC - 1) for kc in range(KC)]
                mm_group(items, an(0, KC), [k])
                ctx.op("act", lambda a, ps=ps, lo=lo, hi=hi: a.activation(out=rt[:, lo:hi], in_=ps[:, 0:hi - lo], func=AF.Sqrt,
                                                                          bias=EPS, scale=1.0 / D),
                       reads=[("ps", k)], writes=[("tmp", 2)])
                psum.free(k)
            ctx.op("dve", lambda v: v.reciprocal(out=rstd[:, lo_all:hi_all], in_=rt[:, lo_all:hi_all]), reads=[("tmp", 2)], writes=["rstd"])
            for c in range(KC):
                ti = tmp_i[0] % 2
                tmp_i[0] += 1
                tb = tmps[ti]
                ctx.op("dve", lambda v, c=c, tb=tb: v.scalar_tensor_tensor(
                    out=tb[:, lo_all:hi_all], in0=xT[:, c, lo_all:hi_all], scalar=Afn(c), in1=rstd[:, lo_all:hi_all],
                    op0=ALU.mult, op1=ALU.mult), reads=[("x", c), "rstd"], writes=[("tmp", ti)])
                ctx.op("act", lambda a, c=c, tb=tb: a.activation(out=hT[:, c, lo_all:hi_all], in_=tb[:, lo_all:hi_all],
                                                                 func=AF.Identity, bias=Bfn(c), scale=1.0),
                       reads=[("tmp", ti)], writes=[("h", c)])

        HN = [("h", c) for c in range(KC)]
        sg_i = [0]

        def ffn(l, ranges):
            lo_all, hi_all = ranges[0][0], ranges[-1][1]
            gv = ffn_gate[l].rearrange("(kc p) f -> p kc f", p=128)
            uv = ffn_up[l].rearrange("(kc p) f -> p kc f", p=128)
            for stg in range(FCN // 2):
                rg, wg = ring.load(gv[:, :, stg * 256:(stg + 1) * 256], KC, 256)
                ru, wu = ring.load(uv[:, :, stg * 256:(stg + 1) * 256], KC, 256)
                if stg == 3:
                    flush_cc()
                for sub in range(2):
                    fc = stg * 2 + sub
                    kg = [psum.alloc() for _ in ranges]
                    items = []
                    for kc in range(KC):
                        for ri, (lo, hi) in enumerate(ranges):
                            items.append((banks[kg[ri]][:, 0:hi - lo], wg[:, kc, sub * 128:(sub + 1) * 128], hT[:, kc, lo:hi],
                                          kc == 0, kc == KC - 1))
                    mm_group(items, [rg] + HN, kg)
                    ku = [psum.alloc() for _ in ranges]
                    items = []
                    for kc in range(KC):
                        for ri, (lo, hi) in enumerate(ranges):
                            items.append((banks[ku[ri]][:, 0:hi - lo], wu[:, kc, sub * 128:(sub + 1) * 128], hT[:, kc, lo:hi],
                                          kc == 0, kc == KC - 1))
                    mm_group(items, [ru] + HN, ku)
                    si = sg_i[0] % 2
                    sg_i[0] += 1
                    sgb = sgs[si]
                    for ri, (lo, hi) in enumerate(ranges):
                        ctx.op("act", lambda a, p=banks[kg[ri]], lo=lo, hi=hi, sgb=sgb: a.activation(
                            out=sgb[:, lo:hi], in_=p[:, 0:hi - lo], func=AF.Silu),
                            reads=[("ps", kg[ri])], writes=[("sg", si)])
                        ctx.op("dve", lambda v, p=banks[ku[ri]], lo=lo, hi=hi, sgb=sgb, fc=fc: v.tensor_tensor(
                            out=aTr[:, fc, lo:hi], in0=sgb[:, lo:hi], in1=p[:, 0:hi - lo], op=ALU.mult),
                            reads=[("sg", si), ("ps", ku[ri])], writes=[("a", fc)])
                    for k in kg + ku:
                        psum.free(k)
            dv = ffn_down[l].rearrange("(fc p) d -> p fc d", p=128)
            for dp in range(8):
                ks = [[psum.alloc() for _ in ranges] for _ in range(2)]
                for (f0, nf) in ((0, 16), (16, 16), (32, 12)):
                    rd, wd = ring.load(dv[:, f0:f0 + nf, dp * 256:(dp + 1) * 256], nf, 256)
                    items = []
                    for sub in range(2):
                        for fi in range(nf):
                            fc = f0 + fi
                            for ri, (lo, hi) in enumerate(ranges):
                                items.append((banks[ks[sub][ri]][:, 0:hi - lo], wd[:, fi, sub * 128:(sub + 1) * 128],
                                              aTr[:, fc, lo:hi], fc == 0, fc == FCN - 1))
                    mm_group(items, [rd] + an(f0, f0 + nf), [k for s_ in ks for k in s_])
                for sub in range(2):
                    dc = dp * 2 + sub
                    for ri, (lo, hi) in enumerate(ranges):
                        ctx.op("dve", lambda v, p=banks[ks[sub][ri]], lo=lo, hi=hi, dc=dc: v.scalar_tensor_tensor(
                            out=xT[:, dc, lo:hi], in0=p[:, 0:hi - lo], scalar=MV(l, 5, dc), in1=xT[:, dc, lo:hi],
                            op0=ALU.mult, op1=ALU.add), reads=[("ps", ks[sub][ri])], writes=[("x", dc)])
                for s_ in ks:
                    for k in s_:
                        psum.free(k)

        def trig_tables(t):
            P2, P3, P4, P5 = sgs[0][0:64, 0:T], sgs[1][0:64, 0:T], tmps[0][0:64, 0:T], tmps[1][0:64, 0:T]
            ctx.op("sp", lambda s: s.dma_start(out=posi[:], in_=posb_d[t]), writes=["posi"], dma=ctx.dsem("posi"))
            ctx.op("dve", lambda v: v.tensor_copy(out=P3, in_=posi[:]), reads=["posi"], writes=[("sg", 1)])
            ctx.op("dve", lambda v: v.tensor_scalar(out=P2, in0=P3, scalar1=V("invf")[0:64, :], scalar2=None,
                                                    op0=ALU.mult), reads=[("sg", 1)], writes=[("sg", 0)])

            def reduce_and_sin(shift, dst, signed):
                ctx.op("dve", lambda v: v.tensor_scalar(out=P4, in0=P2, scalar1=1.0 / TWO_PI,
                                                        scalar2=shift / TWO_PI, op0=ALU.mult, op1=ALU.add),
                       reads=[("sg", 0)], writes=[("tmp", 0)])
                ctx.op("dve", lambda v: v.tensor_copy(out=posi[:], in_=P4), reads=[("tmp", 0)], writes=["posi"])
                ctx.op("dve", lambda v: v.tensor_copy(out=P3, in_=posi[:]), reads=["posi"], writes=[("sg", 1)])
                C1 = 6.28125
                C2 = TWO_PI - C1
                ctx.op("dve", lambda v: v.scalar_tensor_tensor(out=P4, in0=P3, scalar=-C1, in1=P2,
                                                               op0=ALU.mult, op1=ALU.add), reads=[("sg", 1), ("sg", 0)], writes=[("tmp", 0)])
                ctx.op("dve", lambda v: v.scalar_tensor_tensor(out=P4, in0=P3, scalar=-C2, in1=P4,
                                                               op0=ALU.mult, op1=ALU.add), reads=[("sg", 1)], writes=[("tmp", 0)])
                if shift != 0.0:
                    ctx.op("dve", lambda v: v.tensor_scalar(out=P4, in0=P4, scalar1=shift, scalar2=None,
                                                            op0=ALU.add), writes=[("tmp", 0)])
                for cmp_, sgn in ((ALU.is_gt, -TWO_PI), (ALU.is_lt, TWO_PI)):
                    thr = np.pi if sgn < 0 else -np.pi
                    ctx.op("dve", lambda v, cmp_=cmp_, thr=thr, sgn=sgn: v.tensor_scalar(
                        out=P5, in0=P4, scalar1=float(thr), scalar2=float(sgn), op0=cmp_, op1=ALU.mult),
                        reads=[("tmp", 0)], writes=[("tmp", 1)])
                    ctx.op("dve", lambda v: v.tensor_tensor(out=P4, in0=P4, in1=P5, op=ALU.add),
                           reads=[("tmp", 1)], writes=[("tmp", 0)])
                ctx.op("dve", lambda v: v.tensor_scalar(out=P4, in0=P4, scalar1=3.1415925, scalar2=-3.1415925,
                                                        op0=ALU.min, op1=ALU.max), writes=[("tmp", 0)])
                ctx.op("act", lambda a: a.activation(out=trig[:, dst, :], in_=P4, func=AF.Sin), reads=[("tmp", 0)],
                       writes=[("trig", dst)])
                if signed:
                    ctx.op("dve", lambda v: v.tensor_scalar(out=trig[:, dst, :], in0=trig[:, dst, :], scalar1=V("sgn")[0:64, :],
                                                            scalar2=None, op0=ALU.mult), writes=[("trig", dst)])

            reduce_and_sin(float(np.pi / 2), 0, False)
            reduce_and_sin(0.0, 1, True)

        def rope_combine(kA, kB, dst_ap, dst_name):
            t1 = tmps[0][0:64, 0:T]
            t2 = tmps[1][0:64, 0:T]
            ctx.op("dve", lambda v: v.tensor_tensor(out=t1, in0=banks[kA][0:64, :], in1=trig[:, 0, :], op=ALU.mult),
                   reads=[("ps", kA), ("trig", 0)], writes=[("tmp", 0)])
            ctx.op("dve", lambda v: v.tensor_tensor(out=t2, in0=banks[kB][0:64, :], in1=trig[:, 1, :], op=ALU.mult),
                   reads=[("ps", kB), ("trig", 1)], writes=[("tmp", 1)])
            ctx.op("dve", lambda v: v.tensor_tensor(out=dst_ap, in0=t1, in1=t2, op=ALU.add),
                   reads=[("tmp", 0), ("tmp", 1)], writes=dst_name)

        def small_rms(raw, outT, gname):
            ctx.op("act", lambda a: a.activation(out=sq[:, 0:4, 0:T], in_=raw[:], func=AF.Square), reads=RAW4, writes=an(0, 4))
            k = psum.alloc()
            items = [(banks[k][:, :], ones[:], sq[:, c, 0:T], c == 0, c == 3) for c in range(4)]
            mm_group(items, an(0, 4), [k])
            ctx.op("act", lambda a: a.activation(out=rt[:, 0:T], in_=banks[k][:, :], func=AF.Sqrt, bias=EPS, scale=1.0 / 512),
                   reads=[("ps", k)], writes=[("tmp", 2)])
            psum.free(k)
            ctx.op("dve", lambda v: v.reciprocal(out=rstd[:, 0:T], in_=rt[:, 0:T]), reads=[("tmp", 2)], writes=["rstd"])
            for c in range(4):
                ctx.op("dve", lambda v, c=c: v.scalar_tensor_tensor(out=outT[:, c, :], in0=raw[:, c, :], scalar=V(gname, c),
                                                                    in1=rstd[:, 0:T], op0=ALU.mult, op1=ALU.mult),
                       reads=RAW4 + ["rstd"], writes=NRMT)

        def proj512(wsrc, raw, rawname):
            wv = wsrc.rearrange("(kc p) m -> p kc m", p=128)
            for half in range(2):
                rn, w = ring.load(wv[:, :, half * 256:(half + 1) * 256], KC, 256)
                for sub in range(2):
                    oc = half * 2 + sub
                    k = psum.alloc()
                    items = [(banks[k][:, :], w[:, kc, sub * 128:(sub + 1) * 128], hT[:, kc, HALO:TW], kc == 0, kc == KC - 1)
                             for kc in range(KC)]
                    mm_group(items, [rn] + HN, [k])
                    ctx.op("act", lambda a, k=k, oc=oc: a.activation(out=raw[:, oc, :], in_=banks[k][:, :], func=AF.Copy),
                           reads=[("ps", k)], writes=rawname)
                    psum.free(k)

        MAIN = (HALO, TW)
        HAL = (0, HALO)

        def pool_mixer(l, slot, do_halo):
            for g in range(4):
                w = 2 ** (g + 1)
                hv = hT[:, 4 * g:4 * g + 4, :]
                hn = [("h", c) for c in range(4 * g, 4 * g + 4)]
                cur, curname, valid0 = hv, None, 0
                bufs = [(S0, an(32, 40)), (S1, an(40, 48))]
                bi = 0
                k_ = 1
                while k_ < w:
                    dst, dn = bufs[bi]
                    bi ^= 1
                    v0 = valid0 + k_
                    src = cur
                    rd = hn if curname is None else curname
                    ctx.op("dve", lambda v, dst=dst, src=src, v0=v0, k_=k_: v.tensor_tensor(
                        out=dst[:, :, v0:TW], in0=src[:, :, v0:TW], in1=src[:, :, v0 - k_:TW - k_], op=ALU.add),
                        reads=rd, writes=dn)
                    cur, curname, valid0 = dst, dn, v0
                    k_ *= 2
                dlo = 16
                ctx.op("dve", lambda v, cur=cur, hv=hv, g=g, w=w: v.scalar_tensor_tensor(
                    out=aTr[:, 16 + 4 * g:20 + 4 * g, dlo:TW], in0=cur[:, :, dlo:TW], scalar=1.0 / w, in1=hv[:, :, dlo:TW],
                    op0=ALU.mult, op1=ALU.subtract), reads=curname + hn, writes=an(16 + 4 * g, 20 + 4 * g))
                if slot == 0:
                    ic = aux[:, 32 + 16 * g:48 + 16 * g].unsqueeze(1).broadcast_to([128, 4, 16])
                    tb = tmps[2][:, 0:64].rearrange("p (a b) -> p a b", a=4)
                    ctx.op("dve", lambda v, cur=cur, ic=ic, tb=tb: v.tensor_tensor(out=tb, in0=cur[:, :, HALO:HALO + 16], in1=ic, op=ALU.mult),
                           reads=curname, writes=[("tmp", 2)])
                    ctx.op("dve", lambda v, hv=hv, tb=tb, g=g: v.tensor_tensor(out=aTr[:, 16 + 4 * g:20 + 4 * g, HALO:HALO + 16], in0=tb,
                                                                             in1=hv[:, :, HALO:HALO + 16], op=ALU.subtract),
                           reads=[("tmp", 2)] + hn, writes=an(16 + 4 * g, 20 + 4 * g))
                rn, wp = ring.load(pool_w[l, g].rearrange("(ic p) o -> p ic o", p=128), 4, 512)
                ranges = ([(16, HALO)] if do_halo else []) + [MAIN]
                for oc in range(4):
                    ks = [psum.alloc() for _ in ranges]
                    items = []
                    for ic_ in range(4):
                        for ri, (lo, hi) in enumerate(ranges):
                            items.append((banks[ks[ri]][:, 0:hi - lo], wp[:, ic_, oc * 128:(oc + 1) * 128],
                                          aTr[:, 16 + 4 * g + ic_, lo:hi], ic_ == 0, ic_ == 3))
                    mm_group(items, [rn] + an(16 + 4 * g, 20 + 4 * g), ks)
                    c = 4 * g + oc
                    for ri, (lo, hi) in enumerate(ranges):
                        ctx.op("dve", lambda v, p=banks[ks[ri]], lo=lo, hi=hi, c=c: v.scalar_tensor_tensor(
                            out=xT[:, c, lo:hi], in0=p[:, 0:hi - lo], scalar=DER(l, 2, c), in1=xT[:, c, lo:hi],
                            op0=ALU.mult, op1=ALU.add), reads=[("ps", ks[ri])], writes=[("x", c)])
                    for k in ks:
                        psum.free(k)

        def mask_halo(slot):
            if slot == 0:
                vb = validb[:].unsqueeze(1).broadcast_to([128, KC, HALO])
                ctx.op("dve", lambda v: v.tensor_tensor(out=hT[:, :, 0:HALO], in0=hT[:, :, 0:HALO], in1=vb, op=ALU.mult),
                       reads=HN, writes=HN)

        def shared_kv(t):
            rmsnorm_mod([MAIN], lambda c: der[:, 192 + c:193 + c], lambda c: modv[:, 384 + c:385 + c])
            proj512(w_dkv, raw4, RAW4)
            small_rms(raw4, nrmT, "kvn")
            CK = NRMT
            ukv = w_uk.rearrange("(kc p) m -> p kc m", p=128)
            for hh in range(2):
                rn, w = ring.load(ukv[:, :, hh * 1024:(hh + 1) * 1024], 4, 1024)
                for hi_ in range(8):
                    h = hh * 8 + hi_
                    k = psum.alloc()
                    items = [(banks[k][:, :], w[:, kc, hi_ * 128:(hi_ + 1) * 128], nrmT[:, kc, :], kc == 0, kc == 3) for kc in range(4)]
                    mm_group(items, [rn] + CK, [k])
                    bi = h % 3
                    ctx.op("act", lambda a, k=k, bi=bi: a.activation(out=kbufs[bi][:], in_=banks[k][:, :], func=AF.Copy),
                           reads=[("ps", k)], writes=an(28 + bi, 29 + bi))
                    psum.free(k)
                    ctx.op("sp", lambda s, h=h, bi=bi: s.dma_start(out=kin[t][h * 128:(h + 1) * 128, :], in_=kbufs[bi][:]),
                           reads=an(28 + bi, 29 + bi), dma=ctx.dsem(("kbuf", bi)))
            uvv = w_uv.rearrange("(kc p) m -> p kc m", p=128)
            rv = [ring.load(uvv[:, :, hh * 1024:(hh + 1) * 1024], 4, 1024) for hh in range(2)]
            for tc in range(4):
                vb = vbufs[tc % 2]
                for nb in range(4):
                    rn, w = rv[nb // 2]
                    k = psum.alloc()
                    items = [(banks[k][:, :], nrmT[:, kc, tc * 128:(tc + 1) * 128], w[:, kc, (nb % 2) * 512:(nb % 2 + 1) * 512],
                              kc == 0, kc == 3) for kc in range(4)]
                    mm_group(items, [rn] + CK, [k])
                    ctx.op("act", lambda a, k=k, vb=vb, nb=nb: a.activation(out=vb[:, nb * 512:(nb + 1) * 512], in_=banks[k][:, :], func=AF.Copy),
                           reads=[("ps", k)], writes=an(31 + 4 * (tc % 2), 35 + 4 * (tc % 2)))
                    psum.free(k)
                ctx.op("sp", lambda s, vb=vb, tc=tc: s.dma_start(
                    out=vin[t].rearrange("(h p) (c d) -> h p c d", p=128, d=128)[:, :, tc, :].rearrange("h p d -> p h d"),
                    in_=vb[:].rearrange("p (h d) -> p h d", h=H)),
                    reads=an(31 + 4 * (tc % 2), 35 + 4 * (tc % 2)), dma=ctx.dsem(("vbuf", tc % 2)))
            trig_tables(t)
            rn, w = ring.load(w_kr.rearrange("(kc p) m -> p kc m", p=128), KC, 128)
            kA = psum.alloc()
            kB = psum.alloc()
            mm_group([(banks[kA][0:64, :], w[:, kc, 0:64], hT[:, kc, HALO:TW], kc == 0, kc == KC - 1) for kc in range(KC)], [rn] + HN, [kA])
            mm_group([(banks[kB][0:64, :], w[:, kc, 64:128], hT[:, kc, HALO:TW], kc == 0, kc == KC - 1) for kc in range(KC)], [rn] + HN, [kB])
            rope_combine(kA, kB, krbuf, an(39, 40))
            psum.free(kA)
            psum.free(kB)
            ctx.op("sp", lambda s: s.dma_start(out=rin[t], in_=krbuf), reads=an(39, 40), dma=ctx.dsem("krbuf"))

        pending_cc = []

        def queue_cc(t):
            store_names = [("kbuf", 0), ("kbuf", 1), ("kbuf", 2), ("vbuf", 0), ("vbuf", 1), "krbuf"]
            extra = [(ctx.dsems[n][0], ctx.dsems[n][1]) for n in store_names]
            pending_cc.append((t, extra))

        def flush_cc():
            while pending_cc:
                t, extra = pending_cc.pop(0)
                for nm, i_, o_ in (("kout", kin[t], kout[t]), ("vout", vin[t], vout[t]), ("rout", rin[t], rout[t])):
                    ctx.op("pool", lambda g, i_=i_, o_=o_: g.collective_compute("AllGather", ALU.bypass, replica_groups=PAIRS,
                                                                                ins=[i_], outs=[o_]),
                           writes=[(nm, t)], extra=extra)

        if doA:
            for t in range(NT):
                ctx.op("sp", lambda s, t=t: s.dma_start(out=xT[:], in_=xin_d[t]), writes=XN, dma=ctx.dsem("xT"))
                for l in (0, 1):
                    full = [HAL, MAIN]
                    rmsnorm_mod(full, lambda c, l=l: DER(l, 0, c), lambda c, l=l: MV(l, 0, c))
                    mask_halo(t)
                    pool_mixer(l, t, do_halo=(l == 0))
                    fr = full if l == 0 else [MAIN]
                    rmsnorm_mod(fr, lambda c, l=l: DER(l, 1, c), lambda c, l=l: MV(l, 3, c))
                    ffn(l, fr)
                shared_kv(t)
                queue_cc(t)
                ctx.op("sp", lambda s, t=t: s.dma_start(out=x1_d[t], in_=xT[:, :, HALO:TW]), reads=XN, writes=[("x1s", t)],
                       dma=ctx.dsem("xT"))
            flush_cc()

        def attention(l, slot):
            j = l - 2
            nb_ = slot + 1
            nch = 8 * nb_
            rmsnorm_mod([MAIN], lambda c: DER(l, 0, c), lambda c: MV(l, 0, c))
            proj512(w_dq[j], raw4, RAW4)
            small_rms(raw4, nrmT, f"qn{j}")
            CQ = NRMT
            msk = masks[:, slot % 2]

            def qproj(h):
                rn, w = ring.load(w_uq[j, :, h, :].rearrange("(kc p) m -> p kc m", p=128), 4, 256)
                kn = psum.alloc()
                mm_group([(banks[kn][:, :], w[:, kc, 0:128], nrmT[:, kc, :], kc == 0, kc == 3) for kc in range(4)], [rn] + CQ, [kn])
                ctx.op("act", lambda a: a.activation(out=qns[h % 2][:], in_=banks[kn][:, :], func=AF.Copy),
                       reads=[("ps", kn)], writes=an(28 + h % 2, 29 + h % 2))
                psum.free(kn)
                kA = psum.alloc()
                kB = psum.alloc()
                mm_group([(banks[kA][0:64, :], w[:, kc, 128:192], nrmT[:, kc, :], kc == 0, kc == 3) for kc in range(4)], [rn] + CQ, [kA])
                mm_group([(banks[kB][0:64, :], w[:, kc, 192:256], nrmT[:, kc, :], kc == 0, kc == 3) for kc in range(4)], [rn] + CQ, [kB])
                rope_combine(kA, kB, qrs[h % 2], an(30 + h % 2, 31 + h % 2))
                psum.free(kA)
                psum.free(kB)

            def kvload(h):
                ncol = nb_ * T
                def kp(view):
                    return [(view[:, :, ls * T:(ls + 1) * T],
                             kout[ls].rearrange("(r h p) c -> r h p c", r=2, p=128)[:, h].rearrange("r p c -> p r c")) for ls in range(nb_)]

                def vp(view):
                    return [(view[:, :, ls * T:(ls + 1) * T],
                             vout[ls].rearrange("(r h p) c -> r h p c", r=2, p=128)[:, h].rearrange("r p c -> p r c")) for ls in range(nb_)]

                if nb_ <= 2:
                    rk, kvt = ring.load_multi(4, ncol, lambda view: kp(view[:, 0:2, :]) + vp(view[:, 2:4, :]),
                                              reads=[("kout", ls) for ls in range(nb_)] + [("vout", ls) for ls in range(nb_)])
                    return rk, kvt[:, 0:2, :], rk, kvt[:, 2:4, :]
                rk, kt = ring.load_multi(2, ncol, kp, reads=[("kout", ls) for ls in range(nb_)])
                rv_, vt = ring.load_multi(2, ncol, vp, reads=[("vout", ls) for ls in range(nb_)])
                return rk, kt, rv_, vt

            NH = DBG.get("nheads", H)
            if NH == 0:
                return
            qproj(0)
            for h in range(NH):
                rk, kt, rv_, vt = kvload(h)
                if h + 1 < NH:
                    qproj(h + 1)
                qn, qr = qns[h % 2], qrs[h % 2]
                ko = psum.alloc()
                pe_den = (nb_ <= 2)
                kd = psum.alloc() if pe_den else None
                acc = accs[h % 2]
                ACC = an(38 + 2 * (h % 2), 40 + 2 * (h % 2))
                sbank = {}

                def chunk_src(jc):
                    r = jc // (4 * nb_)
                    loc = jc % (4 * nb_)
                    return r, loc

                def emit_qk(jc):
                    r, loc = chunk_src(jc)
                    k = psum.alloc()
                    sbank[jc] = k
                    items = [(banks[k][:, :], kt[:, r, loc * 128:(loc + 1) * 128], qn[:], True, False),
                             (banks[k][:, :], KRb[:, r * nb_ * T + loc * 128:r * nb_ * T + (loc + 1) * 128], qr[:], False, True)]
                    mm_group(items, [rk, "KRb"] + an(28 + h % 2, 29 + h % 2) + an(30 + h % 2, 31 + h % 2), [k])

                def mask_index(jc):
                    r, loc = chunk_src(jc)
                    if loc // 4 == slot:
                        return r * 4 + loc % 4
                    return None

                emit_qk(0)
                emit_qk(1)
                for jc in range(nch):
                    k = sbank.pop(jc)
                    pi_ = jc % 4
                    pT = pTs[pi_]
                    ctx.op("act", lambda a, k=k, pT=pT: a.activation(out=pT[:], in_=banks[k][:, :], func=AF.Exp, scale=SM_SCALE),
                           reads=[("ps", k)], writes=an(32 + pi_, 33 + pi_))
                    psum.free(k)
                    mi = mask_index(jc)
                    if mi is not None:
                        ctx.op("dve", lambda v, pT=pT, mi=mi: v.tensor_tensor(out=pT[:], in0=pT[:], in1=msk[:, mi, :], op=ALU.mult),
                               reads=["masks"], writes=an(32 + pi_, 33 + pi_))
                    r, loc = chunk_src(jc)
                    items = [(banks[ko][:, :], vt[:, r, loc * 128:(loc + 1) * 128], pT[:], jc == 0, jc == nch - 1)]
                    pw = [("ps", ko)]
                    if pe_den:
                        items.append((banks[kd][:, :], ones[:], pT[:], jc == 0, jc == nch - 1))
                        pw.append(("ps", kd))
                    n0 = (jc == 0)
                    for ii, (o, lT, rr, s0, s1) in enumerate(items):
                        fn = (lambda pe, o=o, lT=lT, rr=rr, s0=s0, s1=s1: pe.matmul(o, lT, rr, start=s0, stop=s1))
                        if ii < len(items) - 1:
                            ctx.op("pe", fn, reads=[rv_] + an(32 + pi_, 33 + pi_), writes=(pw if n0 else []), sig=False)
                        else:
                            ctx.op("pe", fn, reads=[rv_, rk] + an(32 + pi_, 33 + pi_), writes=(pw if (n0 or jc == nch - 1) else []))
                    if not pe_den:
                        if jc == 0:
                            ctx.op("dve", lambda v, pT=pT, acc=acc: v.tensor_copy(out=acc, in_=pT[:]), reads=an(32 + pi_, 33 + pi_), writes=ACC)
                        else:
                            ctx.op("dve", lambda v, pT=pT, acc=acc: v.tensor_tensor(out=acc, in0=acc, in1=pT[:], op=ALU.add),
                                   reads=an(32 + pi_, 33 + pi_), writes=ACC)
                    if jc + 2 < nch:
                        emit_qk(jc + 2)
                if not pe_den:
                    kd = psum.alloc()
                    ctx.op("pe", lambda pe, kd=kd, acc=acc: pe.matmul(banks[kd][:, :], ones32[:], acc, start=True, stop=True),
                           reads=ACC + ["ones32"], writes=[("ps", kd)])
                ctx.op("dve", lambda v, kd=kd: v.reciprocal(out=rden[:], in_=banks[kd][:, :]), reads=[("ps", kd)], writes=an(36, 38))
                ctx.op("dve", lambda v, h=h, ko=ko: v.tensor_tensor(out=oTs[:, h, :], in0=banks[ko][:, :], in1=rden[:], op=ALU.mult),
                       reads=[("ps", ko)] + an(36, 38), writes=ON)
                psum.free(ko)
                psum.free(kd)
            if NH < H:
                return
            ov = w_o[j].rearrange("(h p) d -> p h d", p=128)
            for dp in range(8):
                rn, w = ring.load(ov[:, :, dp * 256:(dp + 1) * 256], H, 256)
                for sub in range(2):
                    dc = dp * 2 + sub
                    k = psum.alloc()
                    mm_group([(banks[k][:, :], w[:, h, sub * 128:(sub + 1) * 128], oTs[:, h, :], h == 0, h == H - 1) for h in range(H)],
                             [rn] + ON, [k])
                    ctx.op("dve", lambda v, k=k, dc=dc: v.scalar_tensor_tensor(
                        out=xT[:, dc, HALO:TW], in0=banks[k][:, :], scalar=MV(l, 2, dc), in1=xT[:, dc, HALO:TW],
                        op0=ALU.mult, op1=ALU.add), reads=[("ps", k)], writes=[("x", dc)])
                    psum.free(k)

        def final_norm(t):
            ctx.op("act", lambda a: a.activation(out=sq[:, 0:KC, HALO:TW], in_=xT[:, :, HALO:TW], func=AF.Square), reads=XN, writes=an(0, KC))
            k = psum.alloc()
            mm_group([(banks[k][:, :], ones[:], sq[:, kc, HALO:TW], kc == 0, kc == KC - 1) for kc in range(KC)], an(0, KC), [k])
            ctx.op("act", lambda a: a.activation(out=rt[:, HALO:TW], in_=banks[k][:, :], func=AF.Sqrt, bias=EPS, scale=1.0 / D),
                   reads=[("ps", k)], writes=[("tmp", 2)])
            psum.free(k)
            ctx.op("dve", lambda v: v.reciprocal(out=rstd[:, HALO:TW], in_=rt[:, HALO:TW]), reads=[("tmp", 2)], writes=["rstd"])
            for c in range(KC):
                ctx.op("dve", lambda v, c=c: v.scalar_tensor_tensor(out=xT[:, c, HALO:TW], in0=xT[:, c, HALO:TW], scalar=V("fin", c),
                                                                    in1=rstd[:, HALO:TW], op0=ALU.mult, op1=ALU.mult),
                       reads=["rstd"], writes=[("x", c)])
            ctx.op("sp", lambda s: s.dma_start(out=out_d[t], in_=xT[:, :, HALO:TW]), reads=XN, dma=ctx.dsem("xT"))

        if doB:
            for t in range(NT):
                nb_ = t + 1
                ctx.op("sp", lambda s, t=t: s.dma_start(out=xT[:, :, HALO:TW], in_=x1_d[t]), reads=[("x1s", t)], writes=XN,
                       dma=ctx.dsem("xT"))
                krv = KRb[:, 0:2 * nb_ * T].rearrange("p (r c) -> p r c", r=2)
                for ls in range(nb_):
                    ctx.op("sp", lambda s, ls=ls, krv=krv: s.dma_start(out=krv[:, :, ls * T:(ls + 1) * T],
                                                                      in_=rout[ls].rearrange("(r p) c -> p r c", r=2)),
                           reads=[("rout", ls)], writes=["KRb"], dma=ctx.dsem("KRb"))
                trig_tables(t)
                for l in DBG.get("layersB", (2, 3)):
                    if DBG.get("attn", True):
                        attention(l, t)
                    if DBG.get("ffn", True):
                        rmsnorm_mod([MAIN], lambda c, l=l: DER(l, 1, c), lambda c, l=l: MV(l, 3, c))
                        ffn(l, [MAIN])
                final_norm(t)

        ctx.run(block)
        global LAST_CTX
        LAST_CTX = ctx
    return nc


def _fm(v):
    v = np.asarray(v, dtype=np.float32)
    return np.ascontiguousarray(v.reshape(-1, 128).T)


def _build_vecs(inp):
    vecs = np.zeros((128, NV), np.float32)

    def put(name, arr):
        o, w = VOFF[name]
        vecs[:, o:o + w] = arr

    for l in range(4):
        put(f"modb{l}", _fm(inp["mod_b"][l]))
        put(f"nmix{l}", _fm(inp["norm_mix"][l]))
        put(f"nffn{l}", _fm(inp["norm_ffn"][l]))
    put("kvmodb", _fm(inp["kv_mod_b"]))
    for l in range(2):
        put(f"pscale{l}", _fm(inp["pool_scale"][l]))
    put("kvin", _fm(inp["kv_in_norm"]))
    put("kvn", _fm(inp["kv_norm"]))
    put("qn0", _fm(inp["q_norm"][0]))
    put("qn1", _fm(inp["q_norm"][1]))
    put("fin", _fm(inp["final_norm"]))
    invf = (1.0 / (np.float32(10000.0) ** (np.arange(0, 64, 2, dtype=np.float32) / np.float32(64)))).astype(np.float32)
    col = np.zeros((128, 1), np.float32)
    col[0:32, 0] = invf
    col[32:64, 0] = invf
    put("invf", col)
    sg = np.zeros((128, 1), np.float32)
    sg[0:32] = -1.0
    sg[32:64] = 1.0
    put("sgn", sg)
    return vecs


def _core_inputs_common(inp, vecs):
    cores = []
    wcat = np.concatenate([np.asarray(inp["mod_w"][l], dtype=np.float32) for l in range(4)] + [np.asarray(inp["kv_mod_w"], dtype=np.float32)], axis=1)
    wpar = [np.ascontiguousarray(wcat[:, q * 26624:(q + 1) * 26624]) for q in range(2)]
    for core in range(NCORES):
        b, p = core // 2, core % 2
        d = {"cT": _fm(inp["c"][b]), "vecs": vecs, "modw": wpar[p]}
        posb = np.zeros((NT, 64, T), np.int32)
        for t, blk in enumerate(BLOCKS[p]):
            posb[t] = inp["positions"][b, blk * T:(blk + 1) * T][None, :]
        d["posb"] = posb
        cores.append(d)
    return cores


def _masks(p):
    mk = np.zeros((2, 128, 8, T), np.float32)
    kp = np.arange(128)[:, None]
    qi = np.arange(T)[None, :]
    for par in range(2):
        s = par
        mine, other = BLOCKS[p][s], BLOCKS[1 - p][s]
        for r in range(2):
            for i in range(4):
                if r == p:
                    m = (128 * i + kp <= qi)
                else:
                    m = np.full((128, T), other < mine)
                mk[par, :, r * 4 + i, :] = m
    return mk.astype(ml_dtypes.bfloat16)


def _xin(x, b, p):
    xin = np.zeros((NT, 128, KC, TW), np.float32)
    for t, blk in enumerate(BLOCKS[p]):
        lo = blk * T - HALO
        seg = np.zeros((TW, D), np.float32)
        if lo < 0:
            seg[HALO:] = x[b, 0:T]
        else:
            seg = x[b, lo:lo + TW]
        xin[t] = seg.T.reshape(KC, 128, TW).transpose(1, 0, 2)
    return xin


def _aux(p):
    aux = np.zeros((128, 96), np.float32)
    first = (BLOCKS[p][0] == 0)
    aux[:, 0:HALO] = 0.0 if first else 1.0
    for g in range(4):
        w = 2 ** (g + 1)
        tt = np.arange(16)
        cnt = np.minimum(tt + 1, w) if first else np.full(16, w)
        aux[:, 32 + 16 * g:48 + 16 * g] = (1.0 / cnt.astype(np.float32))[None, :]
    return aux


_NC_CACHE = {}


def _get_nc(phase):
    if phase not in _NC_CACHE:
        _NC_CACHE[phase] = build(phase)
    return _NC_CACHE[phase]


def kernel(**inp):
    inp = {k: np.asarray(v) for k, v in inp.items()}
    x = inp["x"].astype(np.float32, copy=False)
    vecs = _build_vecs(inp)
    common = _core_inputs_common(inp, vecs)
    f32c = lambda a: np.ascontiguousarray(a, dtype=np.float32)
    shared = {k: f32c(inp[k]) for k in ("ffn_gate", "ffn_up", "ffn_down", "pool_w", "w_dkv", "w_uk", "w_uv", "w_dq", "w_o")}
    wkr = inp["w_kr"]
    shared["w_kr_ext"] = np.ascontiguousarray(np.concatenate([wkr, wkr[:, 32:64], wkr[:, 0:32]], axis=1), dtype=np.float32)
    uq = inp["w_uq"].reshape(2, 512, H, 192)
    shared["w_uq_ext"] = np.ascontiguousarray(np.concatenate([uq, uq[..., 160:192], uq[..., 128:160]], axis=-1), dtype=np.float32)
    maps = []
    for core in range(NCORES):
        b, p = core // 2, core % 2
        d = dict(common[core])
        d.update(shared)
        d.update({"xin": _xin(x, b, p), "aux": _aux(p), "masks": _masks(p)})
        maps.append(d)
    res = run_bass_kernel_spmd(_get_nc("F"), maps, core_ids=list(range(NCORES))).results
    out = np.zeros((B, S, D), np.float32)
    for core in range(NCORES):
        b, p = core // 2, core % 2
        o = res[core]["out"]
        for t, blk in enumerate(BLOCKS[p]):
            out[b, blk * T:(blk + 1) * T, :] = o[t].transpose(1, 0, 2).reshape(D, T).T
    return out
```

```python
import contextlib
import numpy as np
import ml_dtypes
import concourse.bass as bass
import concourse.mybir as mybir
from concourse.bass_utils import run_bass_kernel_spmd

F32 = mybir.dt.float32
BF16 = mybir.dt.bfloat16
I32 = mybir.dt.int32
AF = mybir.ActivationFunctionType
ALU = mybir.AluOpType

NCORES = 8
B = 4
S = 4096
D = 2048
KC = 16
F = 5632
FCN = 44
T = 512
HALO = 32
TW = T + HALO
NT = 4
H = 16
EPS = 1e-6
SM_SCALE = 192.0 ** -0.5
BLOCKS = {0: [0, 3, 4, 7], 1: [1, 2, 5, 6]}
NRING = 7
RING_ELEMS = 4096
TWO_PI = float(2.0 * np.pi)


def vec_layout():
    off = {}
    n = 0

    def add(name, w):
        nonlocal n
        off[name] = (n, w)
        n += w

    for l in range(4):
        add(f"modb{l}", 96)
    add("kvmodb", 32)
    for l in range(4):
        add(f"nmix{l}", 16)
        add(f"nffn{l}", 16)
    for l in range(2):
        add(f"pscale{l}", 16)
    add("kvin", 16)
    add("kvn", 4)
    add("qn0", 4)
    add("qn1", 4)
    add("fin", 16)
    add("invf", 1)
    add("sgn", 1)
    return off, n


VOFF, NV = vec_layout()
DBG = {}


class Sem:
    def __init__(self, h):
        self.h = h


class Dep:
    __slots__ = ("w", "r")

    def __init__(self):
        self.w = None
        self.r = []


class Ctx:
    ENG = ("pe", "act", "dve", "pool", "sp")

    def __init__(self, nc, stack):
        self.nc = nc
        self.stack = stack
        self.q = {e: [] for e in self.ENG}
        self.sem = {}
        self.cnt = {}
        self.waited = {e: {} for e in self.ENG}
        self.nsem = 0
        self.deps = {}
        self.const = set()
        self.dsems = {}
        self.pending = {e: [] for e in self.ENG}
        for e in self.ENG:
            self._rot(e)

    def newsem(self, name):
        self.nsem += 1
        return Sem(self.stack.enter_context(self.nc.semaphore(f"{name}_{self.nsem}")))

    def _rot(self, e):
        self.sem[e] = self.newsem("s" + e)
        self.cnt[e] = 0

    def dsem(self, name):
        if name not in self.dsems:
            self.dsems[name] = [self.newsem("d"), 0]
        return self.dsems[name]

    def barrier(self, tok):
        for e in self.ENG:
            self.pending[e].append(tok)

    def emit(self, e, fn, waits=(), sig=True, dma=None):
        ws = []
        if self.pending[e]:
            waits = list(waits) + self.pending[e]
            self.pending[e] = []
        mx = {}
        for t in waits:
            if t is None:
                continue
            s, v = t
            k = id(s)
            if k not in mx or mx[k][1] < v:
                mx[k] = (s, v)
        for k, (s, v) in mx.items():
            if self.waited[e].get(k, 0) >= v:
                continue
            self.waited[e][k] = v
            ws.append((s, v))
        tok = None
        inc = 1
        if dma is not None:
            dma[1] += 16
            tok = (dma[0], dma[1])
            inc = 16
        elif sig:
            if self.cnt[e] >= 30000:
                self._rot(e)
            self.cnt[e] += 1
            tok = (self.sem[e], self.cnt[e])
        self.q[e].append((fn, ws, tok, inc))
        return tok

    def op(self, e, fn, reads=(), writes=(), extra=(), sig=True, dma=None):
        waits = list(extra)
        for n in reads:
            d = self.deps.setdefault(n, Dep())
            if d.w is not None:
                waits.append(d.w)
        for n in writes:
            d = self.deps.setdefault(n, Dep())
            if d.w is not None:
                waits.append(d.w)
            waits.extend(d.r)
        tok = self.emit(e, fn, waits, sig, dma)
        if tok is not None:
            for n in reads:
                if n not in self.const:
                    self.deps[n].r.append(tok)
            for n in writes:
                d = self.deps[n]
                d.w = tok
                d.r = []
        return tok

    def run(self, block):
        def mk(e):
            def body(eng):
                for fn, ws, tok, inc in self.q[e]:
                    for s, v in ws:
                        eng.wait_ge(s.h, v)
                    ins = fn(eng)
                    if tok is not None:
                        ins.then_inc(tok[0].h, inc)
                if e == "sp":
                    for name, (s, c) in self.dsems.items():
                        if c > 0:
                            eng.wait_ge(s.h, c)
            return body

        block.tensor(mk("pe"))
        block.scalar(mk("act"))
        block.vector(mk("dve"))
        block.gpsimd(mk("pool"))
        block.sync(mk("sp"))


class Psum:
    def __init__(self, banks):
        self.banks = banks
        self.held = [False] * len(banks)
        self.i = 0

    def alloc(self):
        n = len(self.banks)
        for _ in range(n):
            k = self.i % n
            self.i += 1
            if not self.held[k]:
                self.held[k] = True
                return k
        raise RuntimeError("psum exhausted")

    def free(self, k):
        self.held[k] = False


class Ring:
    def __init__(self, ctx, slots):
        self.ctx = ctx
        self.slots = slots
        self.i = 0
        self.gen = [0] * len(slots)

    def load(self, src, a, b, npart=128, reads=()):
        k = self.i % len(self.slots)
        self.i += 1
        self.gen[k] += 1
        assert a * b <= RING_ELEMS
        flat = self.slots[k][0:npart, 0:a * b]
        view = flat.rearrange("p (a b) -> p a b", a=a) if a > 1 else flat
        dst = view
        self.ctx.op("pool", lambda g, d=dst, s=src: g.dma_start(out=d, in_=s),
                    reads=list(reads), writes=[("ring", k)], dma=self.ctx.dsem(("ring", k)))
        return ("ring", k), view

    def load_multi(self, a, b, mk_pieces, reads=(), npart=128):
        ctx = self.ctx
        k = self.i % len(self.slots)
        self.i += 1
        flat = self.slots[k][0:npart, 0:a * b]
        view = flat.rearrange("p (a b) -> p a b", a=a)
        name = ("ring", k)
        d = ctx.deps.setdefault(name, Dep())
        waits = [d.w] + list(d.r)
        for n in reads:
            dn = ctx.deps.setdefault(n, Dep())
            waits.append(dn.w)
        tok = None
        for i, (dst, src) in enumerate(mk_pieces(view)):
            tok = ctx.emit("pool", lambda g, d_=dst, s_=src: g.dma_start(out=d_, in_=s_), waits if i == 0 else (),
                           dma=ctx.dsem(name))
        d.w = tok
        d.r = []
        return name, view


def build(phase):
    doA = phase in ("A", "F")
    doB = phase in ("B", "F")
    nc = bass.Bass("TRN2", target_bir_lowering=False)

    def din(name, shape, dt=F32):
        return nc.dram_tensor(name, list(shape), dt, kind="ExternalInput").ap()

    def dout(name, shape, dt=F32):
        return nc.dram_tensor(name, list(shape), dt, kind="ExternalOutput").ap()

    cT_d = din("cT", [128, KC])
    modw_d = din("modw", [D, 208 * 128])
    modin_d = nc.dram_tensor("modin", [128, 208], F32, kind="Internal").ap()
    modout_d = nc.dram_tensor("modout", [2 * 128, 208], F32, kind="Internal").ap()
    vecs_d = din("vecs", [128, NV])
    posb_d = din("posb", [NT, 64, T], I32)
    ffn_gate = din("ffn_gate", [4, D, F])
    ffn_up = din("ffn_up", [4, D, F])
    ffn_down = din("ffn_down", [4, F, D])
    if doA:
        xin_d = din("xin", [NT, 128, KC, TW])
        aux_d = din("aux", [128, 96])
        pool_w = din("pool_w", [2, 4, 512, 512])
        w_dkv = din("w_dkv", [D, 512])
        w_uk = din("w_uk", [512, D])
        w_uv = din("w_uv", [512, D])
        w_kr = din("w_kr_ext", [D, 128])
    assert phase == "F"
    if doB:
        mk_d = din("masks", [2, 128, 8, T], BF16)
        w_dq = din("w_dq", [2, D, 512])
        w_uq = din("w_uq_ext", [2, 512, H, 256])
        w_o = din("w_o", [2, D, D])
        out_d = dout("out", [NT, 128, KC, T])
    if phase == "F":
        x1_d = nc.dram_tensor("x1s", [NT, 128, KC, T], F32, kind="Internal").ap()
        kin = [nc.dram_tensor(f"kin{t}", [H * 128, T], BF16, kind="Internal").ap() for t in range(NT)]
        vin = [nc.dram_tensor(f"vin{t}", [H * 128, T], BF16, kind="Internal").ap() for t in range(NT)]
        rin = [nc.dram_tensor(f"rin{t}", [64, T], BF16, kind="Internal").ap() for t in range(NT)]
        kout = [nc.dram_tensor(f"kout{t}", [2 * H * 128, T], BF16, kind="Internal").ap() for t in range(NT)]
        vout = [nc.dram_tensor(f"vout{t}", [2 * H * 128, T], BF16, kind="Internal").ap() for t in range(NT)]
        rout = [nc.dram_tensor(f"rout{t}", [128, T], BF16, kind="Internal").ap() for t in range(NT)]
        PAIRS = [[0, 1], [2, 3], [4, 5], [6, 7]]

    layers = ([0, 1] if doA else []) + ([2, 3] if doB else [])

    with contextlib.ExitStack() as st:
        ctx = Ctx(nc, st)

        def sb(name, shape, dt):
            return st.enter_context(nc.sbuf_tensor("sb_" + name, list(shape), dt))

        ones = sb("ones", [128, 128], BF16)
        ones32 = sb("ones32", [128, 128], F32)
        vecs = sb("vecs", [128, NV], F32)
        cT = sb("cTs", [128, KC], F32)
        scall = sb("scall", [128, KC], BF16)
        modv = sb("modv", [128, 4 * 96 + 32], F32)
        der = sb("der", [128, 4 * 48 + 16], F32)
        xT = sb("xT", [128, KC, TW], F32)
        hT = sb("hT", [128, KC, TW], BF16)
        aTr = sb("aT", [128, 48, TW], BF16)
        rstd = sb("rstd", [128, TW], F32)
        tmps = [sb(f"tmp{i}", [128, TW], F32) for i in range(3)]
        sgs = [sb(f"sg{i}", [128, TW], F32) for i in range(2)]
        ring_slots = [sb(f"ring{i}", [128, RING_ELEMS], BF16) for i in range(NRING)]
        trig = sb("trig", [64, 2, T], F32)
        posi = sb("posi", [64, T], I32)
        if doA:
            aux = sb("aux", [128, 96], F32)
            validb = sb("validb", [128, HALO], BF16)
        if doB:
            masks = sb("masks", [128, 2, 8, T], BF16)
            KRb = sb("KRb", [128, RING_ELEMS], BF16)
        banks = [st.enter_context(nc.psum_tensor(f"ps{i}", [128, 512], F32)) for i in range(8)]
        psum = Psum(banks)
        ring = Ring(ctx, ring_slots)
        block = st.enter_context(nc.Block())

        sq = aTr
        S0 = aTr[:, 32:40, :].rearrange("p a b -> p (a b)").bitcast(F32).rearrange("p (a b) -> p a b", a=4)
        S1 = aTr[:, 40:48, :].rearrange("p a b -> p (a b)").bitcast(F32).rearrange("p (a b) -> p a b", a=4)

        def an(lo, hi):
            return [("a", i) for i in range(lo, hi)]

        def arow(r0, nr, nparts=128):
            return aTr[0:nparts, r0:r0 + nr, :].rearrange("p a b -> p (a b)")

        raw4 = arow(16, 8)[:, 0:4096].bitcast(F32).rearrange("p (a b) -> p a b", a=4)
        RAW4 = an(16, 24)
        nrmT = arow(24, 4)[:, 0:2048].rearrange("p (a b) -> p a b", a=4)
        NRMT = an(24, 28)
        kbufs = [arow(28 + i, 1)[:, 0:T] for i in range(3)]
        vbufs = [arow(31 + 4 * i, 4)[:, 0:D] for i in range(2)]
        krbuf = arow(39, 1, 64)[:, 0:T]
        qns = [arow(28 + i, 1)[:, 0:T] for i in range(2)]
        qrs = [arow(30 + i, 1, 64)[:, 0:T] for i in range(2)]
        qrf = [arow(30 + i, 1)[:, 0:T] for i in range(2)]
        pTs = [arow(32 + i, 1)[:, 0:T] for i in range(4)]
        rden = arow(36, 2)[:, 0:2 * T].bitcast(F32)
        oTs = arow(0, 16)[:, 0:H * T].rearrange("p (h t) -> p h t", h=H)
        ON = an(0, 16)
        rt = tmps[2]
        accs = [arow(38 + 2 * i, 2)[:, 0:2 * T].bitcast(F32) for i in range(2)]
        gall = arow(40, 7)[:, 0:2 * 2 * 208].bitcast(F32).rearrange("p (r x) -> p r x", r=2)
        gsb = arow(47, 1)[:, 0:416].bitcast(F32)

        XN = [("x", c) for c in range(KC)]

        def V(name, c=None):
            o, w = VOFF[name]
            if c is None:
                return vecs[:, o:o + w]
            return vecs[:, o + c:o + c + 1]

        ctx.op("dve", lambda v: v.memset(ones[:], 1.0), writes=["ones"])
        ctx.op("dve", lambda v: v.memset(ones32[:], 1.0), writes=["ones32"])
        ctx.op("sp", lambda s: s.dma_start(out=vecs[:], in_=vecs_d), writes=["vecs"], dma=ctx.dsem("vecs"))
        ctx.op("sp", lambda s: s.dma_start(out=cT[:], in_=cT_d), writes=["cT"], dma=ctx.dsem("cT"))
        if doA:
            ctx.op("sp", lambda s: s.dma_start(out=aux[:], in_=aux_d), writes=["aux"], dma=ctx.dsem("aux"))
            ctx.op("dve", lambda v: v.tensor_copy(out=validb[:], in_=aux[:, 0:HALO]), reads=["aux"], writes=["validb"])
            ctx.op("dve", lambda v: v.memset(aTr[:, 16:32, :], 0.0), writes=an(16, 32))
        if doB:
            ctx.op("sp", lambda s: s.dma_start(out=masks[:].rearrange("p m c t -> p m (c t)"),
                                               in_=mk_d.rearrange("m p c t -> p m (c t)")),
                   writes=["masks"], dma=ctx.dsem("masks"))
        ctx.op("act", lambda a: a.activation(out=scall[:], in_=cT[:], func=AF.Silu), reads=["cT"], writes=["scall"])
        if doB:
            ctx.op("dve", lambda v: v.memset(KRb[:], 0.0), writes=["KRb"])
        for n in ("ones", "ones32", "vecs", "scall", "aux", "validb", "masks", "modv", "der"):
            ctx.const.add(n)

        def mm_group(items, reads, k_list):
            n = len(items)
            for i, (o, l, r, s0, s1) in enumerate(items):
                fn = (lambda pe, o=o, l=l, r=r, s0=s0, s1=s1: pe.matmul(o, l, r, start=s0, stop=s1))
                if n == 1:
                    ctx.op("pe", fn, reads=reads, writes=[("ps", k) for k in k_list])
                elif i == 0:
                    waits = []
                    for nm in reads:
                        d = ctx.deps.setdefault(nm, Dep())
                        if d.w is not None:
                            waits.append(d.w)
                    for k in k_list:
                        d = ctx.deps.setdefault(("ps", k), Dep())
                        if d.w is not None:
                            waits.append(d.w)
                        waits.extend(d.r)
                    ctx.emit("pe", fn, waits, sig=False)
                elif i == n - 1:
                    ctx.op("pe", fn, reads=reads, writes=[("ps", k) for k in k_list])
                else:
                    ctx.emit("pe", fn, sig=False)

        def mod_all():
            k = psum.alloc()
            ps = banks[k]
            wv = modw_d.rearrange("(kc p) m -> p kc m", p=128)
            for mb in range(104):
                rn, w = ring.load(wv[:, :, mb * 256:(mb + 1) * 256], KC, 256)
                for sub in range(2):
                    lc = mb * 2 + sub
                    items = [(ps[:, lc:lc + 1], w[:, kc, sub * 128:(sub + 1) * 128], scall[:, kc:kc + 1], kc == 0, kc == KC - 1)
                             for kc in range(KC)]
                    mm_group(items, [rn, "scall"], [k])
            ctx.op("dve", lambda v: v.tensor_copy(out=gsb, in_=ps[:, 0:208]), reads=[("ps", k)], writes=an(47, 48))
            psum.free(k)
            ctx.op("sp", lambda s_: s_.dma_start(out=modin_d, in_=gsb), reads=an(47, 48), writes=["modin"], dma=ctx.dsem("gsb"))
            ctx.op("pool", lambda g: g.collective_compute("AllGather", ALU.bypass, replica_groups=[[0, 1], [2, 3], [4, 5], [6, 7]],
                                                          ins=[modin_d], outs=[modout_d]), reads=["modin"], writes=["modout"])
            ctx.op("sp", lambda s_: s_.dma_start(out=gall, in_=modout_d.rearrange("(r p) x -> p r x", p=128)),
                   reads=["modout"], writes=an(40, 47), dma=ctx.dsem("gall"))
            ctx.op("dve", lambda v: v.tensor_tensor(out=modv[:, 0:416], in0=gall.rearrange("p r x -> p (r x)"), in1=vecs[:, 0:416], op=ALU.add),
                   reads=an(40, 47) + ["vecs"], writes=[("modv", 0)])

        def MV(l, part, c=None):
            o = l * 96 + part * 16
            if c is None:
                return modv[:, o:o + 16]
            return modv[:, o + c:o + c + 1]

        def DER(l, part, c=None):
            o = l * 48 + part * 16
            if c is None:
                return der[:, o:o + 16]
            return der[:, o + c:o + c + 1]

        def derive(l):
            for part, vn, sp_ in ((0, f"nmix{l}", 1), (1, f"nffn{l}", 4)):
                ctx.op("dve", lambda v, p=part, vn=vn, sp_=sp_: v.scalar_tensor_tensor(
                    out=DER(l, p), in0=MV(l, sp_), scalar=1.0, in1=V(vn), op0=ALU.add, op1=ALU.mult),
                    reads=[("modv", 0), "vecs"], writes=[("der", l, part)])
            if l < 2:
                ctx.op("dve", lambda v: v.tensor_tensor(out=DER(l, 2), in0=MV(l, 2), in1=V(f"pscale{l}"), op=ALU.mult),
                       reads=[("modv", 0), "vecs"], writes=[("der", l, 2)])

        mod_all()
        for l in layers:
            derive(l)
        ctx.op("dve", lambda v: v.scalar_tensor_tensor(
            out=der[:, 192:208], in0=modv[:, 400:416], scalar=1.0, in1=V("kvin"), op0=ALU.add, op1=ALU.mult),
            reads=[("modv", 0), "vecs"], writes=[("der", "kv")])
        ctx.barrier(ctx.op("dve", lambda v: v.memset(rt[:], 1.0), writes=[("tmp", 2)]))

        tmp_i = [0]

        def rmsnorm_mod(ranges, Afn, Bfn, xreads=None):
            lo_all, hi_all = ranges[0][0], ranges[-1][1]
            ctx.op("act", lambda a: a.activation(out=sq[:, 0:KC, lo_all:hi_all], in_=xT[:, :, lo_all:hi_all], func=AF.Square),
                   reads=XN, writes=an(0, KC))
            for (lo, hi) in ranges:
                k = psum.alloc()
                ps = banks[k]
                items = [(ps[:, 0:hi - lo], ones[:], sq[:, kc, lo:hi], kc == 0, kc == KC - 1) for kc in range(KC)]
                mm_group(items, an(0, KC), [k])
                ctx.op("act", lambda a, ps=ps, lo=lo, hi=hi: a.activation(out=rt[:, lo:hi], in_=ps[:, 0:hi - lo], func=AF.Sqrt,
                                                                          bias=EPS, scale=1.0 / D),
                       reads=[("ps", k)], writes=[("tmp", 2)])
                psum.free(k)
            ctx.op("dve", lambda v: v.reciprocal(out=rstd[:, lo_all:hi_all], in_=rt[:, lo_all:hi_all]), reads=[("tmp", 2)], writes=["rstd"])
            for c in range(KC):
                ti = tmp_i[0] % 2
                tmp_i[0] += 1
                tb = tmps[ti]
                ctx.op("dve", lambda v, c=c, tb=tb: v.scalar_tensor_tensor(
                    out=tb[:, lo_all:hi_all], in0=xT[:, c, lo_all:hi_all], scalar=Afn(c), in1=rstd[:, lo_all:hi_all],
                    op0=ALU.mult, op1=ALU.mult), reads=[("x", c), "rstd"], writes=[("tmp", ti)])
                ctx.op("act", lambda a, c=c, tb=tb: a.activation(out=hT[:, c, lo_all:hi_all], in_=tb[:, lo_all:hi_all],
                                                                 func=AF.Identity, bias=Bfn(c), scale=1.0),
                       reads=[("tmp", ti)], writes=[("h", c)])

        HN = [("h", c) for c in range(KC)]
        sg_i = [0]

        def ffn(l, ranges):
            lo_all, hi_all = ranges[0][0], ranges[-1][1]
            gv = ffn_gate[l].rearrange("(kc p) f -> p kc f", p=128)
            uv = ffn_up[l].rearrange("(kc p) f -> p kc f", p=128)
            for stg in range(FCN // 2):
                rg, wg = ring.load(gv[:, :, stg * 256:(stg + 1) * 256], KC, 256)
                ru, wu = ring.load(uv[:, :, stg * 256:(stg + 1) * 256], KC, 256)
                if stg == 3:
                    flush_cc()
                for sub in range(2):
                    fc = stg * 2 + sub
                    kg = [psum.alloc() for _ in ranges]
                    items = []
                    for kc in range(KC):
                        for ri, (lo, hi) in enumerate(ranges):
                            items.append((banks[kg[ri]][:, 0:hi - lo], wg[:, kc, sub * 128:(sub + 1) * 128], hT[:, kc, lo:hi],
                                          kc == 0, kc == KC - 1))
                    mm_group(items, [rg] + HN, kg)
                    ku = [psum.alloc() for _ in ranges]
                    items = []
                    for kc in range(KC):
                        for ri, (lo, hi) in enumerate(ranges):
                            items.append((banks[ku[ri]][:, 0:hi - lo], wu[:, kc, sub * 128:(sub + 1) * 128], hT[:, kc, lo:hi],
                                          kc == 0, kc == KC - 1))
                    mm_group(items, [ru] + HN, ku)
                    si = sg_i[0] % 2
                    sg_i[0] += 1
                    sgb = sgs[si]
                    for ri, (lo, hi) in enumerate(ranges):
                        ctx.op("act", lambda a, p=banks[kg[ri]], lo=lo, hi=hi, sgb=sgb: a.activation(
                            out=sgb[:, lo:hi], in_=p[:, 0:hi - lo], func=AF.Silu),
                            reads=[("ps", kg[ri])], writes=[("sg", si)])
                        ctx.op("dve", lambda v, p=banks[ku[ri]], lo=lo, hi=hi, sgb=sgb, fc=fc: v.tensor_tensor(
                            out=aTr[:, fc, lo:hi], in0=sgb[:, lo:hi], in1=p[:, 0:hi - lo], op=ALU.mult),
                            reads=[("sg", si), ("ps", ku[ri])], writes=[("a", fc)])
                    for k in kg + ku:
                        psum.free(k)
            dv = ffn_down[l].rearrange("(fc p) d -> p fc d", p=128)
            for dp in range(8):
                ks = [[psum.alloc() for _ in ranges] for _ in range(2)]
                for (f0, nf) in ((0, 16), (16, 16), (32, 12)):
                    rd, wd = ring.load(dv[:, f0:f0 + nf, dp * 256:(dp + 1) * 256], nf, 256)
                    items = []
                    for sub in range(2):
                        for fi in range(nf):
                            fc = f0 + fi
                            for ri, (lo, hi) in enumerate(ranges):
                                items.append((banks[ks[sub][ri]][:, 0:hi - lo], wd[:, fi, sub * 128:(sub + 1) * 128],
                                              aTr[:, fc, lo:hi], fc == 0, fc == FCN - 1))
                    mm_group(items, [rd] + an(f0, f0 + nf), [k for s_ in ks for k in s_])
                for sub in range(2):
                    dc = dp * 2 + sub
                    for ri, (lo, hi) in enumerate(ranges):
                        ctx.op("dve", lambda v, p=banks[ks[sub][ri]], lo=lo, hi=hi, dc=dc: v.scalar_tensor_tensor(
                            out=xT[:, dc, lo:hi], in0=p[:, 0:hi - lo], scalar=MV(l, 5, dc), in1=xT[:, dc, lo:hi],
                            op0=ALU.mult, op1=ALU.add), reads=[("ps", ks[sub][ri])], writes=[("x", dc)])
                for s_ in ks:
                    for k in s_:
                        psum.free(k)

        def trig_tables(t):
            P2, P3, P4, P5 = sgs[0][0:64, 0:T], sgs[1][0:64, 0:T], tmps[0][0:64, 0:T], tmps[1][0:64, 0:T]
            ctx.op("sp", lambda s: s.dma_start(out=posi[:], in_=posb_d[t]), writes=["posi"], dma=ctx.dsem("posi"))
            ctx.op("dve", lambda v: v.tensor_copy(out=P3, in_=posi[:]), reads=["posi"], writes=[("sg", 1)])
            ctx.op("dve", lambda v: v.tensor_scalar(out=P2, in0=P3, scalar1=V("invf")[0:64, :], scalar2=None,
                                                    op0=ALU.mult), reads=[("sg", 1)], writes=[("sg", 0)])

            def reduce_and_sin(shift, dst, signed):
                ctx.op("dve", lambda v: v.tensor_scalar(out=P4, in0=P2, scalar1=1.0 / TWO_PI,
                                                        scalar2=shift / TWO_PI, op0=ALU.mult, op1=ALU.add),
                       reads=[("sg", 0)], writes=[("tmp", 0)])
                ctx.op("dve", lambda v: v.tensor_copy(out=posi[:], in_=P4), reads=[("tmp", 0)], writes=["posi"])
                ctx.op("dve", lambda v: v.tensor_copy(out=P3, in_=posi[:]), reads=["posi"], writes=[("sg", 1)])
                C1 = 6.28125
                C2 = TWO_PI - C1
                ctx.op("dve", lambda v: v.scalar_tensor_tensor(out=P4, in0=P3, scalar=-C1, in1=P2,
                                                               op0=ALU.mult, op1=ALU.add), reads=[("sg", 1), ("sg", 0)], writes=[("tmp", 0)])
                ctx.op("dve", lambda v: v.scalar_tensor_tensor(out=P4, in0=P3, scalar=-C2, in1=P4,
                                                               op0=ALU.mult, op1=ALU.add), reads=[("sg", 1)], writes=[("tmp", 0)])
                if shift != 0.0:
                    ctx.op("dve", lambda v: v.tensor_scalar(out=P4, in0=P4, scalar1=shift, scalar2=None,
                                                            op0=ALU.add), writes=[("tmp", 0)])
                for cmp_, sgn in ((ALU.is_gt, -TWO_PI), (ALU.is_lt, TWO_PI)):
                    thr = np.pi if sgn < 0 else -np.pi
                    ctx.op("dve", lambda v, cmp_=cmp_, thr=thr, sgn=sgn: v.tensor_scalar(
                        out=P5, in0=P4, scalar1=float(thr), scalar2=float(sgn), op0=cmp_, op1=ALU.mult),
                        reads=[("tmp", 0)], writes=[("tmp", 1)])
                    ctx.op("dve", lambda v: v.tensor_tensor(out=P4, in0=P4, in1=P5, op=ALU.add),
                           reads=[("tmp", 1)], writes=[("tmp", 0)])
                ctx.op("dve", lambda v: v.tensor_scalar(out=P4, in0=P4, scalar1=3.1415925, scalar2=-3.1415925,
                                                        op0=ALU.min, op1=ALU.max), writes=[("tmp", 0)])
                ctx.op("act", lambda a: a.activation(out=trig[:, dst, :], in_=P4, func=AF.Sin), reads=[("tmp", 0)],
                       writes=[("trig", dst)])
                if signed:
                    ctx.op("dve", lambda v: v.tensor_scalar(out=trig[:, dst, :], in0=trig[:, dst, :], scalar1=V("sgn")[0:64, :],
                                                            scalar2=None, op0=ALU.mult), writes=[("trig", dst)])

            reduce_and_sin(float(np.pi / 2), 0, False)
            reduce_and_sin(0.0, 1, True)

        def rope_combine(kA, kB, dst_ap, dst_name):
            t1 = tmps[0][0:64, 0:T]
            t2 = tmps[1][0:64, 0:T]
            ctx.op("dve", lambda v: v.tensor_tensor(out=t1, in0=banks[kA][0:64, :], in1=trig[:, 0, :], op=ALU.mult),
                   reads=[("ps", kA), ("trig", 0)], writes=[("tmp", 0)])
            ctx.op("dve", lambda v: v.tensor_tensor(out=t2, in0=banks[kB][0:64, :], in1=trig[:, 1, :], op=ALU.mult),
                   reads=[("ps", kB), ("trig", 1)], writes=[("tmp", 1)])
            ctx.op("dve", lambda v: v.tensor_tensor(out=dst_ap, in0=t1, in1=t2, op=ALU.add),
                   reads=[("tmp", 0), ("tmp", 1)], writes=dst_name)

        def small_rms(raw, outT, gname):
            ctx.op("act", lambda a: a.activation(out=sq[:, 0:4, 0:T], in_=raw[:], func=AF.Square), reads=RAW4, writes=an(0, 4))
            k = psum.alloc()
            items = [(banks[k][:, :], ones[:], sq[:, c, 0:T], c == 0, c == 3) for c in range(4)]
            mm_group(items, an(0, 4), [k])
            ctx.op("act", lambda a: a.activation(out=rt[:, 0:T], in_=banks[k][:, :], func=AF.Sqrt, bias=EPS, scale=1.0 / 512),
                   reads=[("ps", k)], writes=[("tmp", 2)])
            psum.free(k)
            ctx.op("dve", lambda v: v.reciprocal(out=rstd[:, 0:T], in_=rt[:, 0:T]), reads=[("tmp", 2)], writes=["rstd"])
            for c in range(4):
                ctx.op("dve", lambda v, c=c: v.scalar_tensor_tensor(out=outT[:, c, :], in0=raw[:, c, :], scalar=V(gname, c),
                                                                    in1=rstd[:, 0:T], op0=ALU.mult, op1=ALU.mult),
                       reads=RAW4 + ["rstd"], writes=NRMT)

        def proj512(wsrc, raw, rawname):
            wv = wsrc.rearrange("(kc p) m -> p kc m", p=128)
            for half in range(2):
                rn, w = ring.load(wv[:, :, half * 256:(half + 1) * 256], KC, 256)
                for sub in range(2):
                    oc = half * 2 + sub
                    k = psum.alloc()
                    items = [(banks[k][:, :], w[:, kc, sub * 128:(sub + 1) * 128], hT[:, kc, HALO:TW], kc == 0, kc == KC - 1)
                             for kc in range(KC)]
                    mm_group(items, [rn] + HN, [k])
                    ctx.op("act", lambda a, k=k, oc=oc: a.activation(out=raw[:, oc, :], in_=banks[k][:, :], func=AF.Copy),
                           reads=[("ps", k)], writes=rawname)
                    psum.free(k)

        MAIN = (HALO, TW)
        HAL = (0, HALO)

        def pool_mixer(l, slot, do_halo):
            for g in range(4):
                w = 2 ** (g + 1)
                hv = hT[:, 4 * g:4 * g + 4, :]
                hn = [("h", c) for c in range(4 * g, 4 * g + 4)]
                cur, curname, valid0 = hv, None, 0
                bufs = [(S0, an(32, 40)), (S1, an(40, 48))]
                bi = 0
                k_ = 1
                while k_ < w:
                    dst, dn = bufs[bi]
                    bi ^= 1
                    v0 = valid0 + k_
                    src = cur
                    rd = hn if curname is None else curname
                    ctx.op("dve", lambda v, dst=dst, src=src, v0=v0, k_=k_: v.tensor_tensor(
                        out=dst[:, :, v0:TW], in0=src[:, :, v0:TW], in1=src[:, :, v0 - k_:TW - k_], op=ALU.add),
                        reads=rd, writes=dn)
                    cur, curname, valid0 = dst, dn, v0
                    k_ *= 2
                dlo = 16
                ctx.op("dve", lambda v, cur=cur, hv=hv, g=g, w=w: v.scalar_tensor_tensor(
                    out=aTr[:, 16 + 4 * g:20 + 4 * g, dlo:TW], in0=cur[:, :, dlo:TW], scalar=1.0 / w, in1=hv[:, :, dlo:TW],
                    op0=ALU.mult, op1=ALU.subtract), reads=curname + hn, writes=an(16 + 4 * g, 20 + 4 * g))
                if slot == 0:
                    ic = aux[:, 32 + 16 * g:48 + 16 * g].unsqueeze(1).broadcast_to([128, 4, 16])
                    tb = tmps[2][:, 0:64].rearrange("p (a b) -> p a b", a=4)
                    ctx.op("dve", lambda v, cur=cur, ic=ic, tb=tb: v.tensor_tensor(out=tb, in0=cur[:, :, HALO:HALO + 16], in1=ic, op=ALU.mult),
                           reads=curname, writes=[("tmp", 2)])
                    ctx.op("dve", lambda v, hv=hv, tb=tb, g=g: v.tensor_tensor(out=aTr[:, 16 + 4 * g:20 + 4 * g, HALO:HALO + 16], in0=tb,
                                                                             in1=hv[:, :, HALO:HALO + 16], op=ALU.subtract),
                           reads=[("tmp", 2)] + hn, writes=an(16 + 4 * g, 20 + 4 * g))
                rn, wp = ring.load(pool_w[l, g].rearrange("(ic p) o -> p ic o", p=128), 4, 512)
                ranges = ([(16, HALO)] if do_halo else []) + [MAIN]
                for oc in range(4):
                    ks = [psum.alloc() for _ in ranges]
                    items = []
                    for ic_ in range(4):
                        for ri, (lo, hi) in enumerate(ranges):
                            items.append((banks[ks[ri]][:, 0:hi - lo], wp[:, ic_, oc * 128:(oc + 1) * 128],
                                          aTr[:, 16 + 4 * g + ic_, lo:hi], ic_ == 0, ic_ == 3))
                    mm_group(items, [rn] + an(16 + 4 * g, 20 + 4 * g), ks)
                    c = 4 * g + oc
                    for ri, (lo, hi) in enumerate(ranges):
                        ctx.op("dve", lambda v, p=banks[ks[ri]], lo=lo, hi=hi, c=c: v.scalar_tensor_tensor(
                            out=xT[:, c, lo:hi], in0=p[:, 0:hi - lo], scalar=DER(l, 2, c), in1=xT[:, c, lo:hi],
                            op0=ALU.mult, op1=ALU.add), reads=[("ps", ks[ri])], writes=[("x", c)])
                    for k in ks:
                        psum.free(k)

        def mask_halo(slot):
            if slot == 0:
                vb = validb[:].unsqueeze(1).broadcast_to([128, KC, HALO])
                ctx.op("dve", lambda v: v.tensor_tensor(out=hT[:, :, 0:HALO], in0=hT[:, :, 0:HALO], in1=vb, op=ALU.mult),
                       reads=HN, writes=HN)

        def shared_kv(t):
            rmsnorm_mod([MAIN], lambda c: der[:, 192 + c:193 + c], lambda c: modv[:, 384 + c:385 + c])
            proj512(w_dkv, raw4, RAW4)
            small_rms(raw4, nrmT, "kvn")
            CK = NRMT
            ukv = w_uk.rearrange("(kc p) m -> p kc m", p=128)
            for hh in range(2):
                rn, w = ring.load(ukv[:, :, hh * 1024:(hh + 1) * 1024], 4, 1024)
                for hi_ in range(8):
                    h = hh * 8 + hi_
                    k = psum.alloc()
                    items = [(banks[k][:, :], w[:, kc, hi_ * 128:(hi_ + 1) * 128], nrmT[:, kc, :], kc == 0, kc == 3) for kc in range(4)]
                    mm_group(items, [rn] + CK, [k])
                    bi = h % 3
                    ctx.op("act", lambda a, k=k, bi=bi: a.activation(out=kbufs[bi][:], in_=banks[k][:, :], func=AF.Copy),
                           reads=[("ps", k)], writes=an(28 + bi, 29 + bi))
                    psum.free(k)
                    ctx.op("sp", lambda s, h=h, bi=bi: s.dma_start(out=kin[t][h * 128:(h + 1) * 128, :], in_=kbufs[bi][:]),
                           reads=an(28 + bi, 29 + bi), dma=ctx.dsem(("kbuf", bi)))
            uvv = w_uv.rearrange("(kc p) m -> p kc m", p=128)
            rv = [ring.load(uvv[:, :, hh * 1024:(hh + 1) * 1024], 4, 1024) for hh in range(2)]
            for tc in range(4):
                vb = vbufs[tc % 2]
                for nb in range(4):
                    rn, w = rv[nb // 2]
                    k = psum.alloc()
                    items = [(banks[k][:, :], nrmT[:, kc, tc * 128:(tc + 1) * 128], w[:, kc, (nb % 2) * 512:(nb % 2 + 1) * 512],
                              kc == 0, kc == 3) for kc in range(4)]
                    mm_group(items, [rn] + CK, [k])
                    ctx.op("act", lambda a, k=k, vb=vb, nb=nb: a.activation(out=vb[:, nb * 512:(nb + 1) * 512], in_=banks[k][:, :], func=AF.Copy),
                           reads=[("ps", k)], writes=an(31 + 4 * (tc % 2), 35 + 4 * (tc % 2)))
                    psum.free(k)
                ctx.op("sp", lambda s, vb=vb, tc=tc: s.dma_start(
                    out=vin[t].rearrange("(h p) (c d) -> h p c d", p=128, d=128)[:, :, tc, :].rearrange("h p d -> p h d"),
                    in_=vb[:].rearrange("p (h d) -> p h d", h=H)),
                    reads=an(31 + 4 * (tc % 2), 35 + 4 * (tc % 2)), dma=ctx.dsem(("vbuf", tc % 2)))
            trig_tables(t)
            rn, w = ring.load(w_kr.rearrange("(kc p) m -> p kc m", p=128), KC, 128)
            kA = psum.alloc()
            kB = psum.alloc()
            mm_group([(banks[kA][0:64, :], w[:, kc, 0:64], hT[:, kc, HALO:TW], kc == 0, kc == KC - 1) for kc in range(KC)], [rn] + HN, [kA])
            mm_group([(banks[kB][0:64, :], w[:, kc, 64:128], hT[:, kc, HALO:TW], kc == 0, kc == KC - 1) for kc in range(KC)], [rn] + HN, [kB])
            rope_combine(kA, kB, krbuf, an(39, 40))
            psum.free(kA)
            psum.free(kB)
            ctx.op("sp", lambda s: s.dma_start(out=rin[t], in_=krbuf), reads=an(39, 40), dma=ctx.dsem("krbuf"))

        pending_cc = []

        def queue_cc(t):
            store_names = [("kbuf", 0), ("kbuf", 1), ("kbuf", 2), ("vbuf", 0), ("vbuf", 1), "krbuf"]
            extra = [(ctx.dsems[n][0], ctx.dsems[n][1]) for n in store_names]
            pending_cc.append((t, extra))

        def flush_cc():
            while pending_cc:
                t, extra = pending_cc.pop(0)
                for nm, i_, o_ in (("kout", kin[t], kout[t]), ("vout", vin[t], vout[t]), ("rout", rin[t], rout[t])):
                    ctx.op("pool", lambda g, i_=i_, o_=o_: g.collective_compute("AllGather", ALU.bypass, replica_groups=PAIRS,
                                                                                ins=[i_], outs=[o_]),
                           writes=[(nm, t)], extra=extra)

        if doA:
            for t in range(NT):
                ctx.op("sp", lambda s, t=t: s.dma_start(out=xT[:], in_=xin_d[t]), writes=XN, dma=ctx.dsem("xT"))
                for l in (0, 1):
                    full = [HAL, MAIN]
                    rmsnorm_mod(full, lambda c, l=l: DER(l, 0, c), lambda c, l=l: MV(l, 0, c))
                    mask_halo(t)
                    pool_mixer(l, t, do_halo=(l == 0))
                    fr = full if l == 0 else [MAIN]
                    rmsnorm_mod(fr, lambda c, l=l: DER(l, 1, c), lambda c, l=l: MV(l, 3, c))
                    ffn(l, fr)
                shared_kv(t)
                queue_cc(t)
                ctx.op("sp", lambda s, t=t: s.dma_start(out=x1_d[t], in_=xT[:, :, HALO:TW]), reads=XN, writes=[("x1s", t)],
                       dma=ctx.dsem("xT"))
            flush_cc()

        def attention(l, slot):
            j = l - 2
            nb_ = slot + 1
            nch = 8 * nb_
            rmsnorm_mod([MAIN], lambda c: DER(l, 0, c), lambda c: MV(l, 0, c))
            proj512(w_dq[j], raw4, RAW4)
            small_rms(raw4, nrmT, f"qn{j}")
            CQ = NRMT
            msk = masks[:, slot % 2]
            for i_ in range(2):
                ctx.op("dve", lambda v, i_=i_: v.memset(qrf[i_][64:128, :], 0.0), writes=an(30 + i_, 31 + i_))

            def qproj(h):
                rn, w = ring.load(w_uq[j, :, h, :].rearrange("(kc p) m -> p kc m", p=128), 4, 256)
                kn = psum.alloc()
                mm_group([(banks[kn][:, :], w[:, kc, 0:128], nrmT[:, kc, :], kc == 0, kc == 3) for kc in range(4)], [rn] + CQ, [kn])
                ctx.op("act", lambda a: a.activation(out=qns[h % 2][:], in_=banks[kn][:, :], func=AF.Copy),
                       reads=[("ps", kn)], writes=an(28 + h % 2, 29 + h % 2))
                psum.free(kn)
                kA = psum.alloc()
                kB = psum.alloc()
                mm_group([(banks[kA][0:64, :], w[:, kc, 128:192], nrmT[:, kc, :], kc == 0, kc == 3) for kc in range(4)], [rn] + CQ, [kA])
                mm_group([(banks[kB][0:64, :], w[:, kc, 192:256], nrmT[:, kc, :], kc == 0, kc == 3) for kc in range(4)], [rn] + CQ, [kB])
                rope_combine(kA, kB, qrs[h % 2], an(30 + h % 2, 31 + h % 2))
                psum.free(kA)
                psum.free(kB)

            def kvload(h):
                ncol = nb_ * T
                def kp(view):
                    return [(view[:, :, ls * T:(ls + 1) * T],
                             kout[ls].rearrange("(r h p) c -> r h p c", r=2, p=128)[:, h].rearrange("r p c -> p r c")) for ls in range(nb_)]

                def vp(view):
                    return [(view[:, :, ls * T:(ls + 1) * T],
                             vout[ls].rearrange("(r h p) c -> r h p c", r=2, p=128)[:, h].rearrange("r p c -> p r c")) for ls in range(nb_)]

                if nb_ <= 2:
                    rk, kvt = ring.load_multi(4, ncol, lambda view: kp(view[:, 0:2, :]) + vp(view[:, 2:4, :]),
                                              reads=[("kout", ls) for ls in range(nb_)] + [("vout", ls) for ls in range(nb_)])
                    return rk, kvt[:, 0:2, :], rk, kvt[:, 2:4, :]
                rk, kt = ring.load_multi(2, ncol, kp, reads=[("kout", ls) for ls in range(nb_)])
                rv_, vt = ring.load_multi(2, ncol, vp, reads=[("vout", ls) for ls in range(nb_)])
                return rk, kt, rv_, vt

            NH = DBG.get("nheads", H)
            if NH == 0:
                return
            qproj(0)
            for h in range(NH):
                rk, kt, rv_, vt = kvload(h)
                if h + 1 < NH:
                    qproj(h + 1)
                qn, qr = qns[h % 2], qrs[h % 2]
                ko = psum.alloc()
                pe_den = (nb_ <= 2)
                kd = psum.alloc() if pe_den else None
                acc = accs[h % 2]
                ACC = an(38 + 2 * (h % 2), 40 + 2 * (h % 2))
                sbank = {}

                def chunk_src(jc):
                    r = jc // (4 * nb_)
                    loc = jc % (4 * nb_)
                    return r, loc

                def emit_qk(jc):
                    r, loc = chunk_src(jc)
                    k = psum.alloc()
                    sbank[jc] = k
                    items = [(banks[k][:, :], kt[:, r, loc * 128:(loc + 1) * 128], qn[:], True, False),
                             (banks[k][:, :], KRb[:, r * nb_ * T + loc * 128:r * nb_ * T + (loc + 1) * 128], qrf[h % 2], False, True)]
                    mm_group(items, [rk, "KRb"] + an(28 + h % 2, 29 + h % 2) + an(30 + h % 2, 31 + h % 2), [k])

                def mask_index(jc):
                    r, loc = chunk_src(jc)
                    if loc // 4 == slot:
                        return r * 4 + loc % 4
                    return None

                QD = 3
                for jq in range(min(QD, nch)):
                    emit_qk(jq)
                for jc in range(nch):
                    k = sbank.pop(jc)
                    pi_ = jc % 4
                    pT = pTs[pi_]
                    ctx.op("act", lambda a, k=k, pT=pT: a.activation(out=pT[:], in_=banks[k][:, :], func=AF.Exp, scale=SM_SCALE),
                           reads=[("ps", k)], writes=an(32 + pi_, 33 + pi_))
                    psum.free(k)
                    mi = mask_index(jc)
                    if mi is not None:
                        ctx.op("dve", lambda v, pT=pT, mi=mi: v.tensor_tensor(out=pT[:], in0=pT[:], in1=msk[:, mi, :], op=ALU.mult),
                               reads=["masks"], writes=an(32 + pi_, 33 + pi_))
                    r, loc = chunk_src(jc)
                    items = [(banks[ko][:, :], vt[:, r, loc * 128:(loc + 1) * 128], pT[:], jc == 0, jc == nch - 1)]
                    pw = [("ps", ko)]
                    if pe_den:
                        items.append((banks[kd][:, :], ones[:], pT[:], jc == 0, jc == nch - 1))
                        pw.append(("ps", kd))
                    n0 = (jc == 0)
                    for ii, (o, lT, rr, s0, s1) in enumerate(items):
                        fn = (lambda pe, o=o, lT=lT, rr=rr, s0=s0, s1=s1: pe.matmul(o, lT, rr, start=s0, stop=s1))
                        if ii < len(items) - 1:
                            ctx.op("pe", fn, reads=[rv_] + an(32 + pi_, 33 + pi_), writes=(pw if n0 else []), sig=False)
                        else:
                            ctx.op("pe", fn, reads=[rv_, rk] + an(32 + pi_, 33 + pi_), writes=(pw if (n0 or jc == nch - 1) else []))
                    if not pe_den:
                        if jc == 0:
                            ctx.op("dve", lambda v, pT=pT, acc=acc: v.tensor_copy(out=acc, in_=pT[:]), reads=an(32 + pi_, 33 + pi_), writes=ACC)
                        else:
                            ctx.op("dve", lambda v, pT=pT, acc=acc: v.tensor_tensor(out=acc, in0=acc, in1=pT[:], op=ALU.add),
                                   reads=an(32 + pi_, 33 + pi_), writes=ACC)
                    if jc + QD < nch:
                        emit_qk(jc + QD)
                if not pe_den:
                    kd = psum.alloc()
                    ctx.op("pe", lambda pe, kd=kd, acc=acc: pe.matmul(banks[kd][:, :], ones32[:], acc, start=True, stop=True),
                           reads=ACC + ["ones32"], writes=[("ps", kd)])
                ctx.op("dve", lambda v, kd=kd: v.reciprocal(out=rden[:], in_=banks[kd][:, :]), reads=[("ps", kd)], writes=an(36, 38))
                ctx.op("dve", lambda v, h=h, ko=ko: v.tensor_tensor(out=oTs[:, h, :], in0=banks[ko][:, :], in1=rden[:], op=ALU.mult),
                       reads=[("ps", ko)] + an(36, 38), writes=ON)
                psum.free(ko)
                psum.free(kd)
            if NH < H:
                return
            ov = w_o[j].rearrange("(h p) d -> p h d", p=128)
            for dp in range(8):
                rn, w = ring.load(ov[:, :, dp * 256:(dp + 1) * 256], H, 256)
                for sub in range(2):
                    dc = dp * 2 + sub
                    k = psum.alloc()
                    mm_group([(banks[k][:, :], w[:, h, sub * 128:(sub + 1) * 128], oTs[:, h, :], h == 0, h == H - 1) for h in range(H)],
                             [rn] + ON, [k])
                    ctx.op("dve", lambda v, k=k, dc=dc: v.scalar_tensor_tensor(
                        out=xT[:, dc, HALO:TW], in0=banks[k][:, :], scalar=MV(l, 2, dc), in1=xT[:, dc, HALO:TW],
                        op0=ALU.mult, op1=ALU.add), reads=[("ps", k)], writes=[("x", dc)])
                    psum.free(k)

        def final_norm(t):
            ctx.op("act", lambda a: a.activation(out=sq[:, 0:KC, HALO:TW], in_=xT[:, :, HALO:TW], func=AF.Square), reads=XN, writes=an(0, KC))
            k = psum.alloc()
            mm_group([(banks[k][:, :], ones[:], sq[:, kc, HALO:TW], kc == 0, kc == KC - 1) for kc in range(KC)], an(0, KC), [k])
            ctx.op("act", lambda a: a.activation(out=rt[:, HALO:TW], in_=banks[k][:, :], func=AF.Sqrt, bias=EPS, scale=1.0 / D),
                   reads=[("ps", k)], writes=[("tmp", 2)])
            psum.free(k)
            ctx.op("dve", lambda v: v.reciprocal(out=rstd[:, HALO:TW], in_=rt[:, HALO:TW]), reads=[("tmp", 2)], writes=["rstd"])
            for c in range(KC):
                ctx.op("dve", lambda v, c=c: v.scalar_tensor_tensor(out=xT[:, c, HALO:TW], in0=xT[:, c, HALO:TW], scalar=V("fin", c),
                                                                    in1=rstd[:, HALO:TW], op0=ALU.mult, op1=ALU.mult),
                       reads=["rstd"], writes=[("x", c)])
            ctx.op("sp", lambda s: s.dma_start(out=out_d[t], in_=xT[:, :, HALO:TW]), reads=XN, dma=ctx.dsem("xT"))

        if doB:
            for t in range(NT):
                nb_ = t + 1
                ctx.op("sp", lambda s, t=t: s.dma_start(out=xT[:, :, HALO:TW], in_=x1_d[t]), reads=[("x1s", t)], writes=XN,
                       dma=ctx.dsem("xT"))
                krv = KRb[0:64, 0:2 * nb_ * T].rearrange("p (r c) -> p r c", r=2)
                for ls in range(nb_):
                    ctx.op("sp", lambda s, ls=ls, krv=krv: s.dma_start(out=krv[:, :, ls * T:(ls + 1) * T],
                                                                      in_=rout[ls].rearrange("(r p) c -> p r c", r=2)),
                           reads=[("rout", ls)], writes=["KRb"], dma=ctx.dsem("KRb"))
                trig_tables(t)
                for l in DBG.get("layersB", (2, 3)):
                    if DBG.get("attn", True):
                        attention(l, t)
                    if DBG.get("ffn", True):
                        rmsnorm_mod([MAIN], lambda c, l=l: DER(l, 1, c), lambda c, l=l: MV(l, 3, c))
                        ffn(l, [MAIN])
                final_norm(t)

        ctx.run(block)
        global LAST_CTX
        LAST_CTX = ctx
    return nc


def _fm(v):
    v = np.asarray(v, dtype=np.float32)
    return np.ascontiguousarray(v.reshape(-1, 128).T)


def _build_vecs(inp):
    vecs = np.zeros((128, NV), np.float32)

    def put(name, arr):
        o, w = VOFF[name]
        vecs[:, o:o + w] = arr

    for l in range(4):
        put(f"modb{l}", _fm(inp["mod_b"][l]))
        put(f"nmix{l}", _fm(inp["norm_mix"][l]))
        put(f"nffn{l}", _fm(inp["norm_ffn"][l]))
    put("kvmodb", _fm(inp["kv_mod_b"]))
    for l in range(2):
        put(f"pscale{l}", _fm(inp["pool_scale"][l]))
    put("kvin", _fm(inp["kv_in_norm"]))
    put("kvn", _fm(inp["kv_norm"]))
    put("qn0", _fm(inp["q_norm"][0]))
    put("qn1", _fm(inp["q_norm"][1]))
    put("fin", _fm(inp["final_norm"]))
    invf = (1.0 / (np.float32(10000.0) ** (np.arange(0, 64, 2, dtype=np.float32) / np.float32(64)))).astype(np.float32)
    col = np.zeros((128, 1), np.float32)
    col[0:32, 0] = invf
    col[32:64, 0] = invf
    put("invf", col)
    sg = np.zeros((128, 1), np.float32)
    sg[0:32] = -1.0
    sg[32:64] = 1.0
    put("sgn", sg)
    return vecs


def _core_inputs_common(inp, vecs):
    cores = []
    wcat = np.concatenate([np.asarray(inp["mod_w"][l], dtype=np.float32) for l in range(4)] + [np.asarray(inp["kv_mod_w"], dtype=np.float32)], axis=1)
    wpar = [np.ascontiguousarray(wcat[:, q * 26624:(q + 1) * 26624]) for q in range(2)]
    for core in range(NCORES):
        b, p = core // 2, core % 2
        d = {"cT": _fm(inp["c"][b]), "vecs": vecs, "modw": wpar[p]}
        posb = np.zeros((NT, 64, T), np.int32)
        for t, blk in enumerate(BLOCKS[p]):
            posb[t] = inp["positions"][b, blk * T:(blk + 1) * T][None, :]
        d["posb"] = posb
        cores.append(d)
    return cores


def _masks(p):
    mk = np.zeros((2, 128, 8, T), np.float32)
    kp = np.arange(128)[:, None]
    qi = np.arange(T)[None, :]
    for par in range(2):
        s = par
        mine, other = BLOCKS[p][s], BLOCKS[1 - p][s]
        for r in range(2):
            for i in range(4):
                if r == p:
                    m = (128 * i + kp <= qi)
                else:
                    m = np.full((128, T), other < mine)
                mk[par, :, r * 4 + i, :] = m
    return mk.astype(ml_dtypes.bfloat16)


def _xin(x, b, p):
    xin = np.zeros((NT, 128, KC, TW), np.float32)
    for t, blk in enumerate(BLOCKS[p]):
        lo = blk * T - HALO
        seg = np.zeros((TW, D), np.float32)
        if lo < 0:
            seg[HALO:] = x[b, 0:T]
        else:
            seg = x[b, lo:lo + TW]
        xin[t] = seg.T.reshape(KC, 128, TW).transpose(1, 0, 2)
    return xin


def _aux(p):
    aux = np.zeros((128, 96), np.float32)
    first = (BLOCKS[p][0] == 0)
    aux[:, 0:HALO] = 0.0 if first else 1.0
    for g in range(4):
        w = 2 ** (g + 1)
        tt = np.arange(16)
        cnt = np.minimum(tt + 1, w) if first else np.full(16, w)
        aux[:, 32 + 16 * g:48 + 16 * g] = (1.0 / cnt.astype(np.float32))[None, :]
    return aux


_NC_CACHE = {}


def _get_nc(phase):
    if phase not in _NC_CACHE:
        _NC_CACHE[phase] = build(phase)
    return _NC_CACHE[phase]


def kernel(**inp):
    inp = {k: np.asarray(v) for k, v in inp.items()}
    x = inp["x"].astype(np.float32, copy=False)
    vecs = _build_vecs(inp)
    common = _core_inputs_common(inp, vecs)
    f32c = lambda a: np.ascontiguousarray(a, dtype=np.float32)
    shared = {k: f32c(inp[k]) for k in ("ffn_gate", "ffn_up", "ffn_down", "pool_w", "w_dkv", "w_uk", "w_uv", "w_dq", "w_o")}
    wkr = inp["w_kr"]
    shared["w_kr_ext"] = np.ascontiguousarray(np.concatenate([wkr, wkr[:, 32:64], wkr[:, 0:32]], axis=1), dtype=np.float32)
    uq = inp["w_uq"].reshape(2, 512, H, 192)
    shared["w_uq_ext"] = np.ascontiguousarray(np.concatenate([uq, uq[..., 160:192], uq[..., 128:160]], axis=-1), dtype=np.float32)
    maps = []
    for core in range(NCORES):
        b, p = core // 2, core % 2
        d = dict(common[core])
        d.update(shared)
        d.update({"xin": _xin(x, b, p), "aux": _aux(p), "masks": _masks(p)})
        maps.append(d)
    res = run_bass_kernel_spmd(_get_nc("F"), maps, core_ids=list(range(NCORES))).results
    out = np.zeros((B, S, D), np.float32)
    for core in range(NCORES):
        b, p = core // 2, core % 2
        o = res[core]["out"]
        for t, blk in enumerate(BLOCKS[p]):
            out[b, blk * T:(blk + 1) * T, :] = o[t].transpose(1, 0, 2).reshape(D, T).T
    return out
```

```python
import contextlib
import numpy as np
import ml_dtypes
import concourse.bass as bass
import concourse.mybir as mybir
from concourse.bass_utils import run_bass_kernel_spmd

F32 = mybir.dt.float32
BF16 = mybir.dt.bfloat16
I32 = mybir.dt.int32
AF = mybir.ActivationFunctionType
ALU = mybir.AluOpType

NCORES = 8
B = 4
S = 4096
D = 2048
KC = 16
F = 5632
FCN = 44
T = 512
HALO = 32
TW = T + HALO
NT = 4
H = 16
EPS = 1e-6
SM_SCALE = 192.0 ** -0.5
BLOCKS = {0: [0, 3, 4, 7], 1: [1, 2, 5, 6]}
NRING = 7
RING_ELEMS = 4096
TWO_PI = float(2.0 * np.pi)


def vec_layout():
    off = {}
    n = 0

    def add(name, w):
        nonlocal n
        off[name] = (n, w)
        n += w

    for l in range(4):
        add(f"modb{l}", 96)
    add("kvmodb", 32)
    for l in range(4):
        add(f"nmix{l}", 16)
        add(f"nffn{l}", 16)
    for l in range(2):
        add(f"pscale{l}", 16)
    add("kvin", 16)
    add("kvn", 4)
    add("qn0", 4)
    add("qn1", 4)
    add("fin", 16)
    add("invf", 1)
    add("sgn", 1)
    return off, n


VOFF, NV = vec_layout()
DBG = {}


class Sem:
    def __init__(self, h):
        self.h = h


class Dep:
    __slots__ = ("w", "r")

    def __init__(self):
        self.w = None
        self.r = []


class Ctx:
    ENG = ("pe", "act", "dve", "pool", "sp")

    def __init__(self, nc, stack):
        self.nc = nc
        self.stack = stack
        self.q = {e: [] for e in self.ENG}
        self.sem = {}
        self.cnt = {}
        self.waited = {e: {} for e in self.ENG}
        self.nsem = 0
        self.deps = {}
        self.const = set()
        self.dsems = {}
        self.pending = {e: [] for e in self.ENG}
        for e in self.ENG:
            self._rot(e)

    def newsem(self, name):
        self.nsem += 1
        return Sem(self.stack.enter_context(self.nc.semaphore(f"{name}_{self.nsem}")))

    def _rot(self, e):
        self.sem[e] = self.newsem("s" + e)
        self.cnt[e] = 0

    def dsem(self, name):
        if name not in self.dsems:
            self.dsems[name] = [self.newsem("d"), 0]
        return self.dsems[name]

    def barrier(self, tok):
        for e in self.ENG:
            self.pending[e].append(tok)

    def emit(self, e, fn, waits=(), sig=True, dma=None):
        ws = []
        if self.pending[e]:
            waits = list(waits) + self.pending[e]
            self.pending[e] = []
        mx = {}
        for t in waits:
            if t is None:
                continue
            s, v = t
            k = id(s)
            if k not in mx or mx[k][1] < v:
                mx[k] = (s, v)
        for k, (s, v) in mx.items():
            if self.waited[e].get(k, 0) >= v:
                continue
            self.waited[e][k] = v
            ws.append((s, v))
        tok = None
        inc = 1
        if dma is not None:
            dma[1] += 16
            tok = (dma[0], dma[1])
            inc = 16
        elif sig:
            if self.cnt[e] >= 30000:
                self._rot(e)
            self.cnt[e] += 1
            tok = (self.sem[e], self.cnt[e])
        self.q[e].append((fn, ws, tok, inc))
        return tok

    def op(self, e, fn, reads=(), writes=(), extra=(), sig=True, dma=None):
        waits = list(extra)
        for n in reads:
            d = self.deps.setdefault(n, Dep())
            if d.w is not None:
                waits.append(d.w)
        for n in writes:
            d = self.deps.setdefault(n, Dep())
            if d.w is not None:
                waits.append(d.w)
            waits.extend(d.r)
        tok = self.emit(e, fn, waits, sig, dma)
        if tok is not None:
            for n in reads:
                if n not in self.const:
                    self.deps[n].r.append(tok)
            for n in writes:
                d = self.deps[n]
                d.w = tok
                d.r = []
        return tok

    def run(self, block):
        def mk(e):
            def body(eng):
                for fn, ws, tok, inc in self.q[e]:
                    for s, v in ws:
                        eng.wait_ge(s.h, v)
                    ins = fn(eng)
                    if tok is not None:
                        ins.then_inc(tok[0].h, inc)
                if e == "sp":
                    for name, (s, c) in self.dsems.items():
                        if c > 0:
                            eng.wait_ge(s.h, c)
            return body

        block.tensor(mk("pe"))
        block.scalar(mk("act"))
        block.vector(mk("dve"))
        block.gpsimd(mk("pool"))
        block.sync(mk("sp"))


class Psum:
    def __init__(self, banks):
        self.banks = banks
        self.held = [False] * len(banks)
        self.i = 0

    def alloc(self):
        n = len(self.banks)
        for _ in range(n):
            k = self.i % n
            self.i += 1
            if not self.held[k]:
                self.held[k] = True
                return k
        raise RuntimeError("psum exhausted")

    def free(self, k):
        self.held[k] = False


class Ring:
    def __init__(self, ctx, slots):
        self.ctx = ctx
        self.slots = slots
        self.i = 0
        self.gen = [0] * len(slots)

    def load(self, src, a, b, npart=128, reads=()):
        k = self.i % len(self.slots)
        self.i += 1
        self.gen[k] += 1
        assert a * b <= RING_ELEMS
        flat = self.slots[k][0:npart, 0:a * b]
        view = flat.rearrange("p (a b) -> p a b", a=a) if a > 1 else flat
        dst = view
        self.ctx.op("pool", lambda g, d=dst, s=src: g.dma_start(out=d, in_=s),
                    reads=list(reads), writes=[("ring", k)], dma=self.ctx.dsem(("ring", k)))
        return ("ring", k), view

    def load_multi(self, a, b, mk_pieces, reads=(), npart=128):
        ctx = self.ctx
        k = self.i % len(self.slots)
        self.i += 1
        flat = self.slots[k][0:npart, 0:a * b]
        view = flat.rearrange("p (a b) -> p a b", a=a)
        name = ("ring", k)
        d = ctx.deps.setdefault(name, Dep())
        waits = [d.w] + list(d.r)
        for n in reads:
            dn = ctx.deps.setdefault(n, Dep())
            waits.append(dn.w)
        tok = None
        for i, (dst, src) in enumerate(mk_pieces(view)):
            tok = ctx.emit("pool", lambda g, d_=dst, s_=src: g.dma_start(out=d_, in_=s_), waits if i == 0 else (),
                           dma=ctx.dsem(name))
        d.w = tok
        d.r = []
        return name, view


def build(phase):
    doA = phase in ("A", "F")
    doB = phase in ("B", "F")
    nc = bass.Bass("TRN2", target_bir_lowering=False)

    def din(name, shape, dt=F32):
        return nc.dram_tensor(name, list(shape), dt, kind="ExternalInput").ap()

    def dout(name, shape, dt=F32):
        return nc.dram_tensor(name, list(shape), dt, kind="ExternalOutput").ap()

    cT_d = din("cT", [128, KC])
    modw_d = din("modw", [D, 208 * 128])
    modin_d = nc.dram_tensor("modin", [128, 208], F32, kind="Internal").ap()
    modout_d = nc.dram_tensor("modout", [2 * 128, 208], F32, kind="Internal").ap()
    vecs_d = din("vecs", [128, NV])
    posb_d = din("posb", [NT, 64, T], I32)
    ffn_gate = din("ffn_gate", [4, D, F])
    ffn_up = din("ffn_up", [4, D, F])
    ffn_down = din("ffn_down", [4, F, D])
    if doA:
        xin_d = din("xin", [NT, 128, KC, TW])
        aux_d = din("aux", [128, 96])
        pool_w = din("pool_w", [2, 4, 512, 512])
        w_dkv = din("w_dkv", [D, 512])
        w_uk = din("w_uk", [512, D])
        w_uv = din("w_uv", [512, D])
        w_kr = din("w_kr_ext", [D, 128])
    assert phase == "F"
    if doB:
        mk_d = din("masks", [2, 128, 8, T], BF16)
        w_dq = din("w_dq", [2, D, 512])
        w_uq = din("w_uq_ext", [2, 512, H, 384])
        w_o = din("w_o", [2, D, D])
        out_d = dout("out", [NT, 128, KC, T])
    if phase == "F":
        x1_d = nc.dram_tensor("x1s", [NT, 128, KC, T], F32, kind="Internal").ap()
        kin = [nc.dram_tensor(f"kin{t}", [H * 128, T], BF16, kind="Internal").ap() for t in range(NT)]
        vin = [nc.dram_tensor(f"vin{t}", [H * 128, T], BF16, kind="Internal").ap() for t in range(NT)]
        rin = [nc.dram_tensor(f"rin{t}", [64, T], BF16, kind="Internal").ap() for t in range(NT)]
        kout = [nc.dram_tensor(f"kout{t}", [2 * H * 128, T], BF16, kind="Internal").ap() for t in range(NT)]
        vout = [nc.dram_tensor(f"vout{t}", [2 * H * 128, T], BF16, kind="Internal").ap() for t in range(NT)]
        rout = [nc.dram_tensor(f"rout{t}", [128, T], BF16, kind="Internal").ap() for t in range(NT)]
        PAIRS = [[0, 1], [2, 3], [4, 5], [6, 7]]

    layers = ([0, 1] if doA else []) + ([2, 3] if doB else [])

    with contextlib.ExitStack() as st:
        ctx = Ctx(nc, st)

        def sb(name, shape, dt):
            return st.enter_context(nc.sbuf_tensor("sb_" + name, list(shape), dt))

        ones = sb("ones", [128, 128], BF16)
        ones32 = sb("ones32", [128, 128], F32)
        vecs = sb("vecs", [128, NV], F32)
        cT = sb("cTs", [128, KC], F32)
        scall = sb("scall", [128, KC], BF16)
        modv = sb("modv", [128, 4 * 96 + 32], F32)
        der = sb("der", [128, 4 * 48 + 16], F32)
        xT = sb("xT", [128, KC, TW], F32)
        hT = sb("hT", [128, KC, TW], BF16)
        aTr = sb("aT", [128, 48, TW], BF16)
        rstd = sb("rstd", [128, TW], F32)
        tmps = [sb(f"tmp{i}", [128, TW], F32) for i in range(3)]
        sgs = [sb(f"sg{i}", [128, TW], F32) for i in range(2)]
        ring_slots = [sb(f"ring{i}", [128, RING_ELEMS], BF16) for i in range(NRING)]
        trig = sb("trig", [64, 2, T], F32)
        posi = sb("posi", [64, T], I32)
        if doA:
            aux = sb("aux", [128, 96], F32)
            validb = sb("validb", [128, HALO], BF16)
        if doB:
            masks = sb("masks", [128, 2, 8, T], BF16)
            KRb = sb("KRb", [128, RING_ELEMS], BF16)
        banks = [st.enter_context(nc.psum_tensor(f"ps{i}", [128, 512], F32)) for i in range(8)]
        psum = Psum(banks)
        ring = Ring(ctx, ring_slots)
        block = st.enter_context(nc.Block())

        sq = aTr
        S0 = aTr[:, 32:40, :].rearrange("p a b -> p (a b)").bitcast(F32).rearrange("p (a b) -> p a b", a=4)
        S1 = aTr[:, 40:48, :].rearrange("p a b -> p (a b)").bitcast(F32).rearrange("p (a b) -> p a b", a=4)

        def an(lo, hi):
            return [("a", i) for i in range(lo, hi)]

        def arow(r0, nr, nparts=128):
            return aTr[0:nparts, r0:r0 + nr, :].rearrange("p a b -> p (a b)")

        raw4 = arow(16, 8)[:, 0:4096].bitcast(F32).rearrange("p (a b) -> p a b", a=4)
        RAW4 = an(16, 24)
        nrmT = arow(24, 4)[:, 0:2048].rearrange("p (a b) -> p a b", a=4)
        NRMT = an(24, 28)
        kbufs = [arow(28 + i, 1)[:, 0:T] for i in range(3)]
        vbufs = [arow(31 + 4 * i, 4)[:, 0:D] for i in range(2)]
        krbuf = arow(39, 1, 64)[:, 0:T]
        qns = [arow(28 + i, 1)[:, 0:T] for i in range(2)]
        qrs = [arow(30 + i, 1, 64)[:, 0:T] for i in range(2)]
        qrf = [arow(30 + i, 1)[:, 0:T] for i in range(2)]
        pTs = [arow(32 + i, 1)[:, 0:T] for i in range(4)]
        rden = arow(36, 2)[:, 0:2 * T].bitcast(F32)
        oTs = arow(0, 16)[:, 0:H * T].rearrange("p (h t) -> p h t", h=H)
        ON = an(0, 16)
        rt = tmps[2]
        accs = [arow(38 + 2 * i, 2)[:, 0:2 * T].bitcast(F32) for i in range(2)]
        gall = arow(40, 7)[:, 0:2 * 2 * 208].bitcast(F32).rearrange("p (r x) -> p r x", r=2)
        gsb = arow(47, 1)[:, 0:416].bitcast(F32)

        XN = [("x", c) for c in range(KC)]

        def V(name, c=None):
            o, w = VOFF[name]
            if c is None:
                return vecs[:, o:o + w]
            return vecs[:, o + c:o + c + 1]

        ctx.op("dve", lambda v: v.memset(ones[:], 1.0), writes=["ones"])
        ctx.op("dve", lambda v: v.memset(ones32[:], 1.0), writes=["ones32"])
        ctx.op("sp", lambda s: s.dma_start(out=vecs[:], in_=vecs_d), writes=["vecs"], dma=ctx.dsem("vecs"))
        ctx.op("sp", lambda s: s.dma_start(out=cT[:], in_=cT_d), writes=["cT"], dma=ctx.dsem("cT"))
        if doA:
            ctx.op("sp", lambda s: s.dma_start(out=aux[:], in_=aux_d), writes=["aux"], dma=ctx.dsem("aux"))
            ctx.op("dve", lambda v: v.tensor_copy(out=validb[:], in_=aux[:, 0:HALO]), reads=["aux"], writes=["validb"])
            ctx.op("dve", lambda v: v.memset(aTr[:, 16:32, :], 0.0), writes=an(16, 32))
        if doB:
            ctx.op("sp", lambda s: s.dma_start(out=masks[:].rearrange("p m c t -> p m (c t)"),
                                               in_=mk_d.rearrange("m p c t -> p m (c t)")),
                   writes=["masks"], dma=ctx.dsem("masks"))
        ctx.op("act", lambda a: a.activation(out=scall[:], in_=cT[:], func=AF.Silu), reads=["cT"], writes=["scall"])
        if doB:
            ctx.op("dve", lambda v: v.memset(KRb[:], 0.0), writes=["KRb"])
        for n in ("ones", "ones32", "vecs", "scall", "aux", "validb", "masks", "modv", "der"):
            ctx.const.add(n)

        def mm_group(items, reads, k_list, item_reads=None, late=()):
            n = len(items)
            if item_reads is not None:
                def iw(i):
                    ws_ = []
                    for nm in item_reads[i]:
                        d = ctx.deps.setdefault(nm, Dep())
                        if d.w is not None:
                            ws_.append(d.w)
                    return ws_
            for i, (o, l, r, s0, s1) in enumerate(items):
                fn = (lambda pe, o=o, l=l, r=r, s0=s0, s1=s1: pe.matmul(o, l, r, start=s0, stop=s1))
                if n == 1:
                    ctx.op("pe", fn, reads=reads, writes=[("ps", k) for k in k_list])
                elif i == 0:
                    waits = []
                    for nm in reads:
                        d = ctx.deps.setdefault(nm, Dep())
                        if d.w is not None:
                            waits.append(d.w)
                    for k in k_list:
                        d = ctx.deps.setdefault(("ps", k), Dep())
                        if d.w is not None:
                            waits.append(d.w)
                        waits.extend(d.r)
                    if item_reads is not None:
                        waits = waits + iw(0)
                    ctx.emit("pe", fn, waits, sig=False)
                elif i == n - 1:
                    ctx.op("pe", fn, reads=list(reads) + list(late), writes=[("ps", k) for k in k_list])
                else:
                    ctx.emit("pe", fn, iw(i) if item_reads is not None else (), sig=False)

        def mod_all():
            k = psum.alloc()
            ps = banks[k]
            wv = modw_d.rearrange("(kc p) m -> p kc m", p=128)
            for mb in range(104):
                rn, w = ring.load(wv[:, :, mb * 256:(mb + 1) * 256], KC, 256)
                for sub in range(2):
                    lc = mb * 2 + sub
                    items = [(ps[:, lc:lc + 1], w[:, kc, sub * 128:(sub + 1) * 128], scall[:, kc:kc + 1], kc == 0, kc == KC - 1)
                             for kc in range(KC)]
                    mm_group(items, [rn, "scall"], [k])
            ctx.op("dve", lambda v: v.tensor_copy(out=gsb, in_=ps[:, 0:208]), reads=[("ps", k)], writes=an(47, 48))
            psum.free(k)
            ctx.op("sp", lambda s_: s_.dma_start(out=modin_d, in_=gsb), reads=an(47, 48), writes=["modin"], dma=ctx.dsem("gsb"))
            ctx.op("pool", lambda g: g.collective_compute("AllGather", ALU.bypass, replica_groups=[[0, 1], [2, 3], [4, 5], [6, 7]],
                                                          ins=[modin_d], outs=[modout_d]), reads=["modin"], writes=["modout"])
            ctx.op("sp", lambda s_: s_.dma_start(out=gall, in_=modout_d.rearrange("(r p) x -> p r x", p=128)),
                   reads=["modout"], writes=an(40, 47), dma=ctx.dsem("gall"))
            ctx.op("dve", lambda v: v.tensor_tensor(out=modv[:, 0:416], in0=gall.rearrange("p r x -> p (r x)"), in1=vecs[:, 0:416], op=ALU.add),
                   reads=an(40, 47) + ["vecs"], writes=[("modv", 0)])

        def MV(l, part, c=None):
            o = l * 96 + part * 16
            if c is None:
                return modv[:, o:o + 16]
            return modv[:, o + c:o + c + 1]

        def DER(l, part, c=None):
            o = l * 48 + part * 16
            if c is None:
                return der[:, o:o + 16]
            return der[:, o + c:o + c + 1]

        def derive(l):
            for part, vn, sp_ in ((0, f"nmix{l}", 1), (1, f"nffn{l}", 4)):
                ctx.op("dve", lambda v, p=part, vn=vn, sp_=sp_: v.scalar_tensor_tensor(
                    out=DER(l, p), in0=MV(l, sp_), scalar=1.0, in1=V(vn), op0=ALU.add, op1=ALU.mult),
                    reads=[("modv", 0), "vecs"], writes=[("der", l, part)])
            if l < 2:
                ctx.op("dve", lambda v: v.tensor_tensor(out=DER(l, 2), in0=MV(l, 2), in1=V(f"pscale{l}"), op=ALU.mult),
                       reads=[("modv", 0), "vecs"], writes=[("der", l, 2)])

        mod_all()
        for l in layers:
            derive(l)
        ctx.op("dve", lambda v: v.scalar_tensor_tensor(
            out=der[:, 192:208], in0=modv[:, 400:416], scalar=1.0, in1=V("kvin"), op0=ALU.add, op1=ALU.mult),
            reads=[("modv", 0), "vecs"], writes=[("der", "kv")])
        ctx.barrier(ctx.op("dve", lambda v: v.memset(rt[:], 1.0), writes=[("tmp", 2)]))

        tmp_i = [0]

        def rmsnorm_mod(ranges, Afn, Bfn, xreads=None):
            lo_all, hi_all = ranges[0][0], ranges[-1][1]
            ctx.op("act", lambda a: a.activation(out=sq[:, 0:KC, lo_all:hi_all], in_=xT[:, :, lo_all:hi_all], func=AF.Square),
                   reads=XN, writes=an(0, KC))
            for (lo, hi) in ranges:
                k = psum.alloc()
                ps = banks[k]
                items = [(ps[:, 0:hi - lo], ones[:], sq[:, kc, lo:hi], kc == 0, kc == KC - 1) for kc in range(KC)]
                mm_group(items, an(0, KC), [k])
                ctx.op("act", lambda a, ps=ps, lo=lo, hi=hi: a.activation(out=rt[:, lo:hi], in_=ps[:, 0:hi - lo], func=AF.Sqrt,
                                                                          bias=EPS, scale=1.0 / D),
                       reads=[("ps", k)], writes=[("tmp", 2)])
                psum.free(k)
            ctx.op("dve", lambda v: v.reciprocal(out=rstd[:, lo_all:hi_all], in_=rt[:, lo_all:hi_all]), reads=[("tmp", 2)], writes=["rstd"])
            for c in range(KC):
                ti = tmp_i[0] % 2
                tmp_i[0] += 1
                tb = tmps[ti]
                ctx.op("dve", lambda v, c=c, tb=tb: v.scalar_tensor_tensor(
                    out=tb[:, lo_all:hi_all], in0=xT[:, c, lo_all:hi_all], scalar=Afn(c), in1=rstd[:, lo_all:hi_all],
                    op0=ALU.mult, op1=ALU.mult), reads=[("x", c), "rstd"], writes=[("tmp", ti)])
                ctx.op("act", lambda a, c=c, tb=tb: a.activation(out=hT[:, c, lo_all:hi_all], in_=tb[:, lo_all:hi_all],
                                                                 func=AF.Identity, bias=Bfn(c), scale=1.0),
                       reads=[("tmp", ti)], writes=[("h", c)])

        HN = [("h", c) for c in range(KC)]
        sg_i = [0]

        def ffn(l, ranges):
            lo_all, hi_all = ranges[0][0], ranges[-1][1]
            gv = ffn_gate[l].rearrange("(kc p) f -> p kc f", p=128)
            uv = ffn_up[l].rearrange("(kc p) f -> p kc f", p=128)
            for stg in range(FCN // 2):
                rg, wg = ring.load(gv[:, :, stg * 256:(stg + 1) * 256], KC, 256)
                ru, wu = ring.load(uv[:, :, stg * 256:(stg + 1) * 256], KC, 256)
                if stg == 3:
                    flush_cc()
                for sub in range(2):
                    fc = stg * 2 + sub
                    kg = [psum.alloc() for _ in ranges]
                    items = []
                    for kc in range(KC):
                        for ri, (lo, hi) in enumerate(ranges):
                            items.append((banks[kg[ri]][:, 0:hi - lo], wg[:, kc, sub * 128:(sub + 1) * 128], hT[:, kc, lo:hi],
                                          kc == 0, kc == KC - 1))
                    irs = [[("h", kc)] for kc in range(KC) for _ in ranges]
                    mm_group(items, [rg], kg, item_reads=irs, late=HN)
                    ku = [psum.alloc() for _ in ranges]
                    items = []
                    for kc in range(KC):
                        for ri, (lo, hi) in enumerate(ranges):
                            items.append((banks[ku[ri]][:, 0:hi - lo], wu[:, kc, sub * 128:(sub + 1) * 128], hT[:, kc, lo:hi],
                                          kc == 0, kc == KC - 1))
                    mm_group(items, [ru], ku, item_reads=irs, late=HN)
                    si = sg_i[0] % 2
                    sg_i[0] += 1
                    sgb = sgs[si]
                    for ri, (lo, hi) in enumerate(ranges):
                        ctx.op("act", lambda a, p=banks[kg[ri]], lo=lo, hi=hi, sgb=sgb: a.activation(
                            out=sgb[:, lo:hi], in_=p[:, 0:hi - lo], func=AF.Silu),
                            reads=[("ps", kg[ri])], writes=[("sg", si)])
                        ctx.op("dve", lambda v, p=banks[ku[ri]], lo=lo, hi=hi, sgb=sgb, fc=fc: v.tensor_tensor(
                            out=aTr[:, fc, lo:hi], in0=sgb[:, lo:hi], in1=p[:, 0:hi - lo], op=ALU.mult),
                            reads=[("sg", si), ("ps", ku[ri])], writes=[("a", fc)])
                    for k in kg + ku:
                        psum.free(k)
            dv = ffn_down[l].rearrange("(fc p) d -> p fc d", p=128)
            for dp in range(8):
                ks = [[psum.alloc() for _ in ranges] for _ in range(2)]
                for (f0, nf) in ((0, 16), (16, 16), (32, 12)):
                    rd, wd = ring.load(dv[:, f0:f0 + nf, dp * 256:(dp + 1) * 256], nf, 256)
                    items = []
                    for sub in range(2):
                        for fi in range(nf):
                            fc = f0 + fi
                            for ri, (lo, hi) in enumerate(ranges):
                                items.append((banks[ks[sub][ri]][:, 0:hi - lo], wd[:, fi, sub * 128:(sub + 1) * 128],
                                              aTr[:, fc, lo:hi], fc == 0, fc == FCN - 1))
                    mm_group(items, [rd] + an(f0, f0 + nf), [k for s_ in ks for k in s_])
                for sub in range(2):
                    dc = dp * 2 + sub
                    for ri, (lo, hi) in enumerate(ranges):
                        ctx.op("dve", lambda v, p=banks[ks[sub][ri]], lo=lo, hi=hi, dc=dc: v.scalar_tensor_tensor(
                            out=xT[:, dc, lo:hi], in0=p[:, 0:hi - lo], scalar=MV(l, 5, dc), in1=xT[:, dc, lo:hi],
                            op0=ALU.mult, op1=ALU.add), reads=[("ps", ks[sub][ri])], writes=[("x", dc)])
                for s_ in ks:
                    for k in s_:
                        psum.free(k)

        def trig_tables(t):
            P2, P3, P4, P5 = sgs[0][0:64, 0:T], sgs[1][0:64, 0:T], tmps[0][0:64, 0:T], tmps[1][0:64, 0:T]
            ctx.op("sp", lambda s: s.dma_start(out=posi[:], in_=posb_d[t]), writes=["posi"], dma=ctx.dsem("posi"))
            ctx.op("dve", lambda v: v.tensor_copy(out=P3, in_=posi[:]), reads=["posi"], writes=[("sg", 1)])
            ctx.op("dve", lambda v: v.tensor_scalar(out=P2, in0=P3, scalar1=V("invf")[0:64, :], scalar2=None,
                                                    op0=ALU.mult), reads=[("sg", 1)], writes=[("sg", 0)])

            def reduce_and_sin(shift, dst, signed):
                ctx.op("dve", lambda v: v.tensor_scalar(out=P4, in0=P2, scalar1=1.0 / TWO_PI,
                                                        scalar2=shift / TWO_PI, op0=ALU.mult, op1=ALU.add),
                       reads=[("sg", 0)], writes=[("tmp", 0)])
                ctx.op("dve", lambda v: v.tensor_copy(out=posi[:], in_=P4), reads=[("tmp", 0)], writes=["posi"])
                ctx.op("dve", lambda v: v.tensor_copy(out=P3, in_=posi[:]), reads=["posi"], writes=[("sg", 1)])
                C1 = 6.28125
                C2 = TWO_PI - C1
                ctx.op("dve", lambda v: v.scalar_tensor_tensor(out=P4, in0=P3, scalar=-C1, in1=P2,
                                                               op0=ALU.mult, op1=ALU.add), reads=[("sg", 1), ("sg", 0)], writes=[("tmp", 0)])
                ctx.op("dve", lambda v: v.scalar_tensor_tensor(out=P4, in0=P3, scalar=-C2, in1=P4,
                                                               op0=ALU.mult, op1=ALU.add), reads=[("sg", 1)], writes=[("tmp", 0)])
                if shift != 0.0:
                    ctx.op("dve", lambda v: v.tensor_scalar(out=P4, in0=P4, scalar1=shift, scalar2=None,
                                                            op0=ALU.add), writes=[("tmp", 0)])
                for cmp_, sgn in ((ALU.is_gt, -TWO_PI), (ALU.is_lt, TWO_PI)):
                    thr = np.pi if sgn < 0 else -np.pi
                    ctx.op("dve", lambda v, cmp_=cmp_, thr=thr, sgn=sgn: v.tensor_scalar(
                        out=P5, in0=P4, scalar1=float(thr), scalar2=float(sgn), op0=cmp_, op1=ALU.mult),
                        reads=[("tmp", 0)], writes=[("tmp", 1)])
                    ctx.op("dve", lambda v: v.tensor_tensor(out=P4, in0=P4, in1=P5, op=ALU.add),
                           reads=[("tmp", 1)], writes=[("tmp", 0)])
                ctx.op("dve", lambda v: v.tensor_scalar(out=P4, in0=P4, scalar1=3.1415925, scalar2=-3.1415925,
                                                        op0=ALU.min, op1=ALU.max), writes=[("tmp", 0)])
                ctx.op("act", lambda a: a.activation(out=trig[:, dst, :], in_=P4, func=AF.Sin), reads=[("tmp", 0)],
                       writes=[("trig", dst)])
                if signed:
                    ctx.op("dve", lambda v: v.tensor_scalar(out=trig[:, dst, :], in0=trig[:, dst, :], scalar1=V("sgn")[0:64, :],
                                                            scalar2=None, op0=ALU.mult), writes=[("trig", dst)])

            reduce_and_sin(float(np.pi / 2), 0, False)
            reduce_and_sin(0.0, 1, True)

        def rope_combine(kA, kB, dst_ap, dst_name):
            t1 = tmps[0][0:64, 0:T]
            t2 = tmps[1][0:64, 0:T]
            ctx.op("dve", lambda v: v.tensor_tensor(out=t1, in0=banks[kA][0:64, :], in1=trig[:, 0, :], op=ALU.mult),
                   reads=[("ps", kA), ("trig", 0)], writes=[("tmp", 0)])
            ctx.op("dve", lambda v: v.tensor_tensor(out=t2, in0=banks[kB][0:64, :], in1=trig[:, 1, :], op=ALU.mult),
                   reads=[("ps", kB), ("trig", 1)], writes=[("tmp", 1)])
            ctx.op("dve", lambda v: v.tensor_tensor(out=dst_ap, in0=t1, in1=t2, op=ALU.add),
                   reads=[("tmp", 0), ("tmp", 1)], writes=dst_name)

        def small_rms(raw, outT, gname):
            ctx.op("act", lambda a: a.activation(out=sq[:, 0:4, 0:T], in_=raw[:], func=AF.Square), reads=RAW4, writes=an(0, 4))
            k = psum.alloc()
            items = [(banks[k][:, :], ones[:], sq[:, c, 0:T], c == 0, c == 3) for c in range(4)]
            mm_group(items, an(0, 4), [k])
            ctx.op("act", lambda a: a.activation(out=rt[:, 0:T], in_=banks[k][:, :], func=AF.Sqrt, bias=EPS, scale=1.0 / 512),
                   reads=[("ps", k)], writes=[("tmp", 2)])
            psum.free(k)
            ctx.op("dve", lambda v: v.reciprocal(out=rstd[:, 0:T], in_=rt[:, 0:T]), reads=[("tmp", 2)], writes=["rstd"])
            for c in range(4):
                ctx.op("dve", lambda v, c=c: v.scalar_tensor_tensor(out=outT[:, c, :], in0=raw[:, c, :], scalar=V(gname, c),
                                                                    in1=rstd[:, 0:T], op0=ALU.mult, op1=ALU.mult),
                       reads=RAW4 + ["rstd"], writes=NRMT)

        def proj512(wsrc, raw, rawname):
            wv = wsrc.rearrange("(kc p) m -> p kc m", p=128)
            for half in range(2):
                rn, w = ring.load(wv[:, :, half * 256:(half + 1) * 256], KC, 256)
                for sub in range(2):
                    oc = half * 2 + sub
                    k = psum.alloc()
                    items = [(banks[k][:, :], w[:, kc, sub * 128:(sub + 1) * 128], hT[:, kc, HALO:TW], kc == 0, kc == KC - 1)
                             for kc in range(KC)]
                    mm_group(items, [rn], [k], item_reads=[[("h", kc)] for kc in range(KC)], late=HN)
                    ctx.op("act", lambda a, k=k, oc=oc: a.activation(out=raw[:, oc, :], in_=banks[k][:, :], func=AF.Copy),
                           reads=[("ps", k)], writes=rawname)
                    psum.free(k)

        MAIN = (HALO, TW)
        HAL = (0, HALO)

        def pool_mixer(l, slot, do_halo):
            for g in range(4):
                w = 2 ** (g + 1)
                hv = hT[:, 4 * g:4 * g + 4, :]
                hn = [("h", c) for c in range(4 * g, 4 * g + 4)]
                cur, curname, valid0 = hv, None, 0
                bufs = [(S0, an(32, 40)), (S1, an(40, 48))]
                bi = 0
                k_ = 1
                while k_ < w:
                    dst, dn = bufs[bi]
                    bi ^= 1
                    v0 = valid0 + k_
                    src = cur
                    rd = hn if curname is None else curname
                    ctx.op("dve", lambda v, dst=dst, src=src, v0=v0, k_=k_: v.tensor_tensor(
                        out=dst[:, :, v0:TW], in0=src[:, :, v0:TW], in1=src[:, :, v0 - k_:TW - k_], op=ALU.add),
                        reads=rd, writes=dn)
                    cur, curname, valid0 = dst, dn, v0
                    k_ *= 2
                dlo = 16
                ctx.op("dve", lambda v, cur=cur, hv=hv, g=g, w=w: v.scalar_tensor_tensor(
                    out=aTr[:, 16 + 4 * g:20 + 4 * g, dlo:TW], in0=cur[:, :, dlo:TW], scalar=1.0 / w, in1=hv[:, :, dlo:TW],
                    op0=ALU.mult, op1=ALU.subtract), reads=curname + hn, writes=an(16 + 4 * g, 20 + 4 * g))
                if slot == 0:
                    ic = aux[:, 32 + 16 * g:48 + 16 * g].unsqueeze(1).broadcast_to([128, 4, 16])
                    tb = tmps[2][:, 0:64].rearrange("p (a b) -> p a b", a=4)
                    ctx.op("dve", lambda v, cur=cur, ic=ic, tb=tb: v.tensor_tensor(out=tb, in0=cur[:, :, HALO:HALO + 16], in1=ic, op=ALU.mult),
                           reads=curname, writes=[("tmp", 2)])
                    ctx.op("dve", lambda v, hv=hv, tb=tb, g=g: v.tensor_tensor(out=aTr[:, 16 + 4 * g:20 + 4 * g, HALO:HALO + 16], in0=tb,
                                                                             in1=hv[:, :, HALO:HALO + 16], op=ALU.subtract),
                           reads=[("tmp", 2)] + hn, writes=an(16 + 4 * g, 20 + 4 * g))
                rn, wp = ring.load(pool_w[l, g].rearrange("(ic p) o -> p ic o", p=128), 4, 512)
                ranges = ([(16, HALO)] if do_halo else []) + [MAIN]
                for oc in range(4):
                    ks = [psum.alloc() for _ in ranges]
                    items = []
                    for ic_ in range(4):
                        for ri, (lo, hi) in enumerate(ranges):
                            items.append((banks[ks[ri]][:, 0:hi - lo], wp[:, ic_, oc * 128:(oc + 1) * 128],
                                          aTr[:, 16 + 4 * g + ic_, lo:hi], ic_ == 0, ic_ == 3))
                    mm_group(items, [rn] + an(16 + 4 * g, 20 + 4 * g), ks)
                    c = 4 * g + oc
                    for ri, (lo, hi) in enumerate(ranges):
                        ctx.op("dve", lambda v, p=banks[ks[ri]], lo=lo, hi=hi, c=c: v.scalar_tensor_tensor(
                            out=xT[:, c, lo:hi], in0=p[:, 0:hi - lo], scalar=DER(l, 2, c), in1=xT[:, c, lo:hi],
                            op0=ALU.mult, op1=ALU.add), reads=[("ps", ks[ri])], writes=[("x", c)])
                    for k in ks:
                        psum.free(k)

        def mask_halo(slot):
            if slot == 0:
                vb = validb[:].unsqueeze(1).broadcast_to([128, KC, HALO])
                ctx.op("dve", lambda v: v.tensor_tensor(out=hT[:, :, 0:HALO], in0=hT[:, :, 0:HALO], in1=vb, op=ALU.mult),
                       reads=HN, writes=HN)

        def shared_kv(t):
            rmsnorm_mod([MAIN], lambda c: der[:, 192 + c:193 + c], lambda c: modv[:, 384 + c:385 + c])
            proj512(w_dkv, raw4, RAW4)
            small_rms(raw4, nrmT, "kvn")
            CK = NRMT
            ukv = w_uk.rearrange("(kc p) m -> p kc m", p=128)
            for hh in range(2):
                rn, w = ring.load(ukv[:, :, hh * 1024:(hh + 1) * 1024], 4, 1024)
                for hi_ in range(8):
                    h = hh * 8 + hi_
                    k = psum.alloc()
                    items = [(banks[k][:, :], w[:, kc, hi_ * 128:(hi_ + 1) * 128], nrmT[:, kc, :], kc == 0, kc == 3) for kc in range(4)]
                    mm_group(items, [rn] + CK, [k])
                    bi = h % 3
                    ctx.op("act", lambda a, k=k, bi=bi: a.activation(out=kbufs[bi][:], in_=banks[k][:, :], func=AF.Copy),
                           reads=[("ps", k)], writes=an(28 + bi, 29 + bi))
                    psum.free(k)
                    ctx.op("sp", lambda s, h=h, bi=bi: s.dma_start(out=kin[t][h * 128:(h + 1) * 128, :], in_=kbufs[bi][:]),
                           reads=an(28 + bi, 29 + bi), dma=ctx.dsem(("kbuf", bi)))
            uvv = w_uv.rearrange("(kc p) m -> p kc m", p=128)
            rv = [ring.load(uvv[:, :, hh * 1024:(hh + 1) * 1024], 4, 1024) for hh in range(2)]
            for tc in range(4):
                vb = vbufs[tc % 2]
                for nb in range(4):
                    rn, w = rv[nb // 2]
                    k = psum.alloc()
                    items = [(banks[k][:, :], nrmT[:, kc, tc * 128:(tc + 1) * 128], w[:, kc, (nb % 2) * 512:(nb % 2 + 1) * 512],
                              kc == 0, kc == 3) for kc in range(4)]
                    mm_group(items, [rn] + CK, [k])
                    ctx.op("act", lambda a, k=k, vb=vb, nb=nb: a.activation(out=vb[:, nb * 512:(nb + 1) * 512], in_=banks[k][:, :], func=AF.Copy),
                           reads=[("ps", k)], writes=an(31 + 4 * (tc % 2), 35 + 4 * (tc % 2)))
                    psum.free(k)
                ctx.op("sp", lambda s, vb=vb, tc=tc: s.dma_start(
                    out=vin[t].rearrange("(h p) (c d) -> h p c d", p=128, d=128)[:, :, tc, :].rearrange("h p d -> p h d"),
                    in_=vb[:].rearrange("p (h d) -> p h d", h=H)),
                    reads=an(31 + 4 * (tc % 2), 35 + 4 * (tc % 2)), dma=ctx.dsem(("vbuf", tc % 2)))
            trig_tables(t)
            rn, w = ring.load(w_kr.rearrange("(kc p) m -> p kc m", p=128), KC, 128)
            kA = psum.alloc()
            kB = psum.alloc()
            mm_group([(banks[kA][0:64, :], w[:, kc, 0:64], hT[:, kc, HALO:TW], kc == 0, kc == KC - 1) for kc in range(KC)], [rn] + HN, [kA])
            mm_group([(banks[kB][0:64, :], w[:, kc, 64:128], hT[:, kc, HALO:TW], kc == 0, kc == KC - 1) for kc in range(KC)], [rn] + HN, [kB])
            rope_combine(kA, kB, krbuf, an(39, 40))
            psum.free(kA)
            psum.free(kB)
            ctx.op("sp", lambda s: s.dma_start(out=rin[t], in_=krbuf), reads=an(39, 40), dma=ctx.dsem("krbuf"))

        pending_cc = []

        def queue_cc(t):
            store_names = [("kbuf", 0), ("kbuf", 1), ("kbuf", 2), ("vbuf", 0), ("vbuf", 1), "krbuf"]
            extra = [(ctx.dsems[n][0], ctx.dsems[n][1]) for n in store_names]
            pending_cc.append((t, extra))

        def flush_cc():
            while pending_cc:
                t, extra = pending_cc.pop(0)
                for nm, i_, o_ in (("kout", kin[t], kout[t]), ("vout", vin[t], vout[t]), ("rout", rin[t], rout[t])):
                    ctx.op("pool", lambda g, i_=i_, o_=o_: g.collective_compute("AllGather", ALU.bypass, replica_groups=PAIRS,
                                                                                ins=[i_], outs=[o_]),
                           writes=[(nm, t)], extra=extra)

        if doA:
            for t in range(NT):
                ctx.op("sp", lambda s, t=t: s.dma_start(out=xT[:], in_=xin_d[t]), writes=XN, dma=ctx.dsem("xT"))
                for l in (0, 1):
                    full = [HAL, MAIN]
                    rmsnorm_mod(full, lambda c, l=l: DER(l, 0, c), lambda c, l=l: MV(l, 0, c))
                    mask_halo(t)
                    pool_mixer(l, t, do_halo=(l == 0))
                    fr = full if l == 0 else [MAIN]
                    rmsnorm_mod(fr, lambda c, l=l: DER(l, 1, c), lambda c, l=l: MV(l, 3, c))
                    ffn(l, fr)
                shared_kv(t)
                queue_cc(t)
                ctx.op("sp", lambda s, t=t: s.dma_start(out=x1_d[t], in_=xT[:, :, HALO:TW]), reads=XN, writes=[("x1s", t)],
                       dma=ctx.dsem("xT"))
            flush_cc()

        def attention(l, slot):
            j = l - 2
            nb_ = slot + 1
            nch = 8 * nb_
            rmsnorm_mod([MAIN], lambda c: DER(l, 0, c), lambda c: MV(l, 0, c))
            proj512(w_dq[j], raw4, RAW4)
            small_rms(raw4, nrmT, f"qn{j}")
            CQ = NRMT
            msk = masks[:, slot % 2]
            for i_ in range(2):
                ctx.op("dve", lambda v, i_=i_: v.memset(qrf[i_][64:128, :], 0.0), writes=an(30 + i_, 31 + i_))

            def qproj(h):
                rn, w = ring.load(w_uq[j, :, h, :].rearrange("(kc p) m -> p kc m", p=128), 4, 384)
                kn = psum.alloc()
                mm_group([(banks[kn][:, :], w[:, kc, 0:128], nrmT[:, kc, :], kc == 0, kc == 3) for kc in range(4)], [rn] + CQ, [kn])
                ctx.op("act", lambda a: a.activation(out=qns[h % 2][:], in_=banks[kn][:, :], func=AF.Copy),
                       reads=[("ps", kn)], writes=an(28 + h % 2, 29 + h % 2))
                psum.free(kn)
                kA = psum.alloc()
                kB = psum.alloc()
                mm_group([(banks[kA][:, :], w[:, kc, 128:256], nrmT[:, kc, :], kc == 0, kc == 3) for kc in range(4)], [rn] + CQ, [kA])
                mm_group([(banks[kB][:, :], w[:, kc, 256:384], nrmT[:, kc, :], kc == 0, kc == 3) for kc in range(4)], [rn] + CQ, [kB])
                rope_combine(kA, kB, qrs[h % 2], an(30 + h % 2, 31 + h % 2))
                psum.free(kA)
                psum.free(kB)

            def kvload(h):
                ncol = nb_ * T
                def kp(view):
                    return [(view[:, :, ls * T:(ls + 1) * T],
                             kout[ls].rearrange("(r h p) c -> r h p c", r=2, p=128)[:, h].rearrange("r p c -> p r c")) for ls in range(nb_)]

                def vp(view):
                    return [(view[:, :, ls * T:(ls + 1) * T],
                             vout[ls].rearrange("(r h p) c -> r h p c", r=2, p=128)[:, h].rearrange("r p c -> p r c")) for ls in range(nb_)]

                if nb_ <= 2:
                    rk, kvt = ring.load_multi(4, ncol, lambda view: kp(view[:, 0:2, :]) + vp(view[:, 2:4, :]),
                                              reads=[("kout", ls) for ls in range(nb_)] + [("vout", ls) for ls in range(nb_)])
                    return rk, kvt[:, 0:2, :], rk, kvt[:, 2:4, :]
                rk, kt = ring.load_multi(2, ncol, kp, reads=[("kout", ls) for ls in range(nb_)])
                rv_, vt = ring.load_multi(2, ncol, vp, reads=[("vout", ls) for ls in range(nb_)])
                return rk, kt, rv_, vt

            NH = DBG.get("nheads", H)
            if NH == 0:
                return
            qproj(0)
            for h in range(NH):
                rk, kt, rv_, vt = kvload(h)
                if h + 1 < NH:
                    qproj(h + 1)
                qn, qr = qns[h % 2], qrs[h % 2]
                ko = psum.alloc()
                pe_den = (nb_ <= 2)
                kd = psum.alloc() if pe_den else None
                acc = accs[h % 2]
                ACC = an(38 + 2 * (h % 2), 40 + 2 * (h % 2))
                sbank = {}

                def chunk_src(jc):
                    r = jc // (4 * nb_)
                    loc = jc % (4 * nb_)
                    return r, loc

                def emit_qk(jc):
                    r, loc = chunk_src(jc)
                    k = psum.alloc()
                    sbank[jc] = k
                    items = [(banks[k][:, :], kt[:, r, loc * 128:(loc + 1) * 128], qn[:], True, False),
                             (banks[k][:, :], KRb[:, r * nb_ * T + loc * 128:r * nb_ * T + (loc + 1) * 128], qrf[h % 2], False, True)]
                    mm_group(items, [rk, "KRb"] + an(28 + h % 2, 29 + h % 2) + an(30 + h % 2, 31 + h % 2), [k])

                def mask_index(jc):
                    r, loc = chunk_src(jc)
                    if loc // 4 == slot:
                        return r * 4 + loc % 4
                    return None

                QD = 3
                for jq in range(min(QD, nch)):
                    emit_qk(jq)
                for jc in range(nch):
                    k = sbank.pop(jc)
                    pi_ = jc % 4
                    pT = pTs[pi_]
                    ctx.op("act", lambda a, k=k, pT=pT: a.activation(out=pT[:], in_=banks[k][:, :], func=AF.Exp, scale=SM_SCALE),
                           reads=[("ps", k)], writes=an(32 + pi_, 33 + pi_))
                    psum.free(k)
                    mi = mask_index(jc)
                    if mi is not None:
                        ctx.op("dve", lambda v, pT=pT, mi=mi: v.tensor_tensor(out=pT[:], in0=pT[:], in1=msk[:, mi, :], op=ALU.mult),
                               reads=["masks"], writes=an(32 + pi_, 33 + pi_))
                    r, loc = chunk_src(jc)
                    items = [(banks[ko][:, :], vt[:, r, loc * 128:(loc + 1) * 128], pT[:], jc == 0, jc == nch - 1)]
                    pw = [("ps", ko)]
                    if pe_den:
                        items.append((banks[kd][:, :], ones[:], pT[:], jc == 0, jc == nch - 1))
                        pw.append(("ps", kd))
                    n0 = (jc == 0)
                    for ii, (o, lT, rr, s0, s1) in enumerate(items):
                        fn = (lambda pe, o=o, lT=lT, rr=rr, s0=s0, s1=s1: pe.matmul(o, lT, rr, start=s0, stop=s1))
                        if ii < len(items) - 1:
                            ctx.op("pe", fn, reads=[rv_] + an(32 + pi_, 33 + pi_), writes=(pw if n0 else []), sig=False)
                        else:
                            ctx.op("pe", fn, reads=[rv_, rk] + an(32 + pi_, 33 + pi_), writes=(pw if (n0 or jc == nch - 1) else []))
                    if not pe_den:
                        if jc == 0:
                            ctx.op("dve", lambda v, pT=pT, acc=acc: v.tensor_copy(out=acc, in_=pT[:]), reads=an(32 + pi_, 33 + pi_), writes=ACC)
                        else:
                            ctx.op("dve", lambda v, pT=pT, acc=acc: v.tensor_tensor(out=acc, in0=acc, in1=pT[:], op=ALU.add),
                                   reads=an(32 + pi_, 33 + pi_), writes=ACC)
                    if jc + QD < nch:
                        emit_qk(jc + QD)
                if not pe_den:
                    kd = psum.alloc()
                    ctx.op("pe", lambda pe, kd=kd, acc=acc: pe.matmul(banks[kd][:, :], ones32[:], acc, start=True, stop=True),
                           reads=ACC + ["ones32"], writes=[("ps", kd)])
                ctx.op("dve", lambda v, kd=kd: v.reciprocal(out=rden[:], in_=banks[kd][:, :]), reads=[("ps", kd)], writes=an(36, 38))
                ctx.op("dve", lambda v, h=h, ko=ko: v.tensor_tensor(out=oTs[:, h, :], in0=banks[ko][:, :], in1=rden[:], op=ALU.mult),
                       reads=[("ps", ko)] + an(36, 38), writes=ON)
                psum.free(ko)
                psum.free(kd)
            if NH < H:
                return
            ov = w_o[j].rearrange("(h p) d -> p h d", p=128)
            for dp in range(8):
                rn, w = ring.load(ov[:, :, dp * 256:(dp + 1) * 256], H, 256)
                for sub in range(2):
                    dc = dp * 2 + sub
                    k = psum.alloc()
                    mm_group([(banks[k][:, :], w[:, h, sub * 128:(sub + 1) * 128], oTs[:, h, :], h == 0, h == H - 1) for h in range(H)],
                             [rn] + ON, [k])
                    ctx.op("dve", lambda v, k=k, dc=dc: v.scalar_tensor_tensor(
                        out=xT[:, dc, HALO:TW], in0=banks[k][:, :], scalar=MV(l, 2, dc), in1=xT[:, dc, HALO:TW],
                        op0=ALU.mult, op1=ALU.add), reads=[("ps", k)], writes=[("x", dc)])
                    psum.free(k)

        def final_norm(t):
            ctx.op("act", lambda a: a.activation(out=sq[:, 0:KC, HALO:TW], in_=xT[:, :, HALO:TW], func=AF.Square), reads=XN, writes=an(0, KC))
            k = psum.alloc()
            mm_group([(banks[k][:, :], ones[:], sq[:, kc, HALO:TW], kc == 0, kc == KC - 1) for kc in range(KC)], an(0, KC), [k])
            ctx.op("act", lambda a: a.activation(out=rt[:, HALO:TW], in_=banks[k][:, :], func=AF.Sqrt, bias=EPS, scale=1.0 / D),
                   reads=[("ps", k)], writes=[("tmp", 2)])
            psum.free(k)
            ctx.op("dve", lambda v: v.reciprocal(out=rstd[:, HALO:TW], in_=rt[:, HALO:TW]), reads=[("tmp", 2)], writes=["rstd"])
            for c in range(KC):
                ctx.op("dve", lambda v, c=c: v.scalar_tensor_tensor(out=xT[:, c, HALO:TW], in0=xT[:, c, HALO:TW], scalar=V("fin", c),
                                                                    in1=rstd[:, HALO:TW], op0=ALU.mult, op1=ALU.mult),
                       reads=["rstd"], writes=[("x", c)])
            ctx.op("sp", lambda s: s.dma_start(out=out_d[t], in_=xT[:, :, HALO:TW]), reads=XN, dma=ctx.dsem("xT"))

        if doB:
            for t in range(NT):
                nb_ = t + 1
                ctx.op("sp", lambda s, t=t: s.dma_start(out=xT[:, :, HALO:TW], in_=x1_d[t]), reads=[("x1s", t)], writes=XN,
                       dma=ctx.dsem("xT"))
                krv = KRb[0:64, 0:2 * nb_ * T].rearrange("p (r c) -> p r c", r=2)
                for ls in range(nb_):
                    ctx.op("sp", lambda s, ls=ls, krv=krv: s.dma_start(out=krv[:, :, ls * T:(ls + 1) * T],
                                                                      in_=rout[ls].rearrange("(r p) c -> p r c", r=2)),
                           reads=[("rout", ls)], writes=["KRb"], dma=ctx.dsem("KRb"))
                trig_tables(t)
                for l in DBG.get("layersB", (2, 3)):
                    if DBG.get("attn", True):
                        attention(l, t)
                    if DBG.get("ffn", True):
                        rmsnorm_mod([MAIN], lambda c, l=l: DER(l, 1, c), lambda c, l=l: MV(l, 3, c))
                        ffn(l, [MAIN])
                final_norm(t)

        ctx.run(block)
        global LAST_CTX
        LAST_CTX = ctx
    return nc


def _fm(v):
    v = np.asarray(v, dtype=np.float32)
    return np.ascontiguousarray(v.reshape(-1, 128).T)


def _build_vecs(inp):
    vecs = np.zeros((128, NV), np.float32)

    def put(name, arr):
        o, w = VOFF[name]
        vecs[:, o:o + w] = arr

    for l in range(4):
        put(f"modb{l}", _fm(inp["mod_b"][l]))
        put(f"nmix{l}", _fm(inp["norm_mix"][l]))
        put(f"nffn{l}", _fm(inp["norm_ffn"][l]))
    put("kvmodb", _fm(inp["kv_mod_b"]))
    for l in range(2):
        put(f"pscale{l}", _fm(inp["pool_scale"][l]))
    put("kvin", _fm(inp["kv_in_norm"]))
    put("kvn", _fm(inp["kv_norm"]))
    put("qn0", _fm(inp["q_norm"][0]))
    put("qn1", _fm(inp["q_norm"][1]))
    put("fin", _fm(inp["final_norm"]))
    invf = (1.0 / (np.float32(10000.0) ** (np.arange(0, 64, 2, dtype=np.float32) / np.float32(64)))).astype(np.float32)
    col = np.zeros((128, 1), np.float32)
    col[0:32, 0] = invf
    col[32:64, 0] = invf
    put("invf", col)
    sg = np.zeros((128, 1), np.float32)
    sg[0:32] = -1.0
    sg[32:64] = 1.0
    put("sgn", sg)
    return vecs


def _core_inputs_common(inp, vecs):
    cores = []
    wcat = np.concatenate([np.asarray(inp["mod_w"][l], dtype=np.float32) for l in range(4)] + [np.asarray(inp["kv_mod_w"], dtype=np.float32)], axis=1)
    wpar = [np.ascontiguousarray(wcat[:, q * 26624:(q + 1) * 26624]) for q in range(2)]
    for core in range(NCORES):
        b, p = core // 2, core % 2
        d = {"cT": _fm(inp["c"][b]), "vecs": vecs, "modw": wpar[p]}
        posb = np.zeros((NT, 64, T), np.int32)
        for t, blk in enumerate(BLOCKS[p]):
            posb[t] = inp["positions"][b, blk * T:(blk + 1) * T][None, :]
        d["posb"] = posb
        cores.append(d)
    return cores


def _masks(p):
    mk = np.zeros((2, 128, 8, T), np.float32)
    kp = np.arange(128)[:, None]
    qi = np.arange(T)[None, :]
    for par in range(2):
        s = par
        mine, other = BLOCKS[p][s], BLOCKS[1 - p][s]
        for r in range(2):
            for i in range(4):
                if r == p:
                    m = (128 * i + kp <= qi)
                else:
                    m = np.full((128, T), other < mine)
                mk[par, :, r * 4 + i, :] = m
    return mk.astype(ml_dtypes.bfloat16)


def _xin(x, b, p):
    xin = np.zeros((NT, 128, KC, TW), np.float32)
    for t, blk in enumerate(BLOCKS[p]):
        lo = blk * T - HALO
        seg = np.zeros((TW, D), np.float32)
        if lo < 0:
            seg[HALO:] = x[b, 0:T]
        else:
            seg = x[b, lo:lo + TW]
        xin[t] = seg.T.reshape(KC, 128, TW).transpose(1, 0, 2)
    return xin


def _aux(p):
    aux = np.zeros((128, 96), np.float32)
    first = (BLOCKS[p][0] == 0)
    aux[:, 0:HALO] = 0.0 if first else 1.0
    for g in range(4):
        w = 2 ** (g + 1)
        tt = np.arange(16)
        cnt = np.minimum(tt + 1, w) if first else np.full(16, w)
        aux[:, 32 + 16 * g:48 + 16 * g] = (1.0 / cnt.astype(np.float32))[None, :]
    return aux


_NC_CACHE = {}


def _get_nc(phase):
    if phase not in _NC_CACHE:
        _NC_CACHE[phase] = build(phase)
    return _NC_CACHE[phase]


def kernel(**inp):
    inp = {k: np.asarray(v) for k, v in inp.items()}
    x = inp["x"].astype(np.float32, copy=False)
    vecs = _build_vecs(inp)
    common = _core_inputs_common(inp, vecs)
    f32c = lambda a: np.ascontiguousarray(a, dtype=np.float32)
    shared = {k: f32c(inp[k]) for k in ("ffn_gate", "ffn_up", "ffn_down", "pool_w", "w_dkv", "w_uk", "w_uv", "w_dq", "w_o")}
    wkr = inp["w_kr"]
    shared["w_kr_ext"] = np.ascontiguousarray(np.concatenate([wkr, wkr[:, 32:64], wkr[:, 0:32]], axis=1), dtype=np.float32)
    uq = inp["w_uq"].reshape(2, 512, H, 192)
    zpad = np.zeros(uq.shape[:-1] + (64,), np.float32)
    shared["w_uq_ext"] = np.ascontiguousarray(
        np.concatenate([uq, zpad, uq[..., 160:192], uq[..., 128:160], zpad], axis=-1), dtype=np.float32)
    maps = []
    for core in range(NCORES):
        b, p = core // 2, core % 2
        d = dict(common[core])
        d.update(shared)
        d.update({"xin": _xin(x, b, p), "aux": _aux(p), "masks": _masks(p)})
        maps.append(d)
    res = run_bass_kernel_spmd(_get_nc("F"), maps, core_ids=list(range(NCORES))).results
    out = np.zeros((B, S, D), np.float32)
    for core in range(NCORES):
        b, p = core // 2, core % 2
        o = res[core]["out"]
        for t, blk in enumerate(BLOCKS[p]):
            out[b, blk * T:(blk + 1) * T, :] = o[t].transpose(1, 0, 2).reshape(D, T).T
    return out
```
